# Optimizing a Trainium2 kernel written in Bass

```python
import math
import jax, jax.numpy as jnp
from jax import lax
import numpy as np

D_MODEL = 1024
BATCH = 4
SEQ = 4096
DEPTH = 2

FOURIER_GROUPS = 4
FOURIER_GROUP_DIM = D_MODEL // 8
FOURIER_WIDTH = FOURIER_GROUPS * FOURIER_GROUP_DIM
DIFF_HEADS = 4
DIFF_QK_DIM = D_MODEL // 16
DIFF_V_DIM = 2 * DIFF_QK_DIM
DIFF_WIDTH = DIFF_HEADS * DIFF_V_DIM
DIFF_QK_WIDTH = DIFF_HEADS * 2 * DIFF_QK_DIM
EVEN_IN_WIDTH = FOURIER_WIDTH + 2 * DIFF_QK_WIDTH + DIFF_WIDTH
EVEN_MIX_WIDTH = FOURIER_WIDTH + DIFF_WIDTH
Q_BLOCK = 128
REL_BUCKETS = 32
REL_MAX_DIST = 128
SGU_CHUNK = 128
SGU_GROUPS = 8
SGU_WIDTH = D_MODEL
SGU_GROUP_DIM = SGU_WIDTH // SGU_GROUPS
D_FF = ((-(-8 * D_MODEL // 3) + 255) // 256) * 256
RMS_EPS = 1e-6
N_EVEN = (DEPTH + 1) // 2
N_ODD = DEPTH // 2

kernel_name = 'hybrid_fourier_diffattn_sgu_encoder'


def rms_norm(x, g):
    xf = x.astype(jnp.float32)
    y = xf * lax.rsqrt(jnp.mean(xf * xf, axis=-1, keepdims=True) + RMS_EPS)
    return (y * g.astype(jnp.float32)).astype(x.dtype)


def t5_bucket(rel):
    half = REL_BUCKETS // 2
    max_exact = half // 2
    ret = (rel > 0).astype(jnp.int32) * half
    n = jnp.abs(rel)
    nf = jnp.maximum(n, 1).astype(jnp.float32)
    large = max_exact + (jnp.log(nf / max_exact) / math.log(REL_MAX_DIST / max_exact)
                         * (half - max_exact)).astype(jnp.int32)
    large = jnp.minimum(large, half - 1)
    return ret + jnp.where(n < max_exact, n, large)


def fourier_mix(z):
    b, s, _ = z.shape
    zg = z.astype(jnp.float32).reshape(b, s, FOURIER_GROUPS, FOURIER_GROUP_DIM)
    f = jnp.fft.fftn(zg, axes=(1, 3), norm='ortho').real
    return f.reshape(b, s, FOURIER_WIDTH).astype(z.dtype)


def diff_attention(q, k, v, rel_table, lam, lam_init, subln_g):
    b, s = q.shape[0], q.shape[1]
    nb = s // Q_BLOCK
    qb = (q * (DIFF_QK_DIM ** -0.5)).reshape(b, nb, Q_BLOCK, DIFF_HEADS, 2, DIFF_QK_DIM)
    qb = jnp.moveaxis(qb, 1, 0)
    kpos = jnp.arange(s, dtype=jnp.int32)

    def one_block(args):
        qi, bi = args
        qpos = bi * Q_BLOCK + jnp.arange(Q_BLOCK, dtype=jnp.int32)
        bias = rel_table[t5_bucket(kpos[None, :] - qpos[:, None])]
        bias = jnp.transpose(bias, (2, 0, 1)).astype(jnp.float32)
        logits = jnp.einsum('bqhjd,bkhjd->bhjqk', qi, k).astype(jnp.float32)
        p = jax.nn.softmax(logits + bias[None, :, None], axis=-1)
        a = p[:, :, 0] - lam * p[:, :, 1]
        return jnp.einsum('bhqk,bkhe->bqhe', a.astype(v.dtype), v)

    o = lax.map(one_block, (qb, jnp.arange(nb, dtype=jnp.int32)))
    o = jnp.moveaxis(o, 0, 1).reshape(b, s, DIFF_HEADS, DIFF_V_DIM)
    o = rms_norm(o, subln_g) * (1.0 - lam_init)
    return o.reshape(b, s, DIFF_WIDTH)


def even_mixer(h, w_in, w_out, rel_table, lambdas, subln_g, lam_init):
    b, s, _ = h.shape
    z = h @ w_in
    zf = z[..., :FOURIER_WIDTH]
    zq = z[..., FOURIER_WIDTH:FOURIER_WIDTH + DIFF_QK_WIDTH]
    zk = z[..., FOURIER_WIDTH + DIFF_QK_WIDTH:FOURIER_WIDTH + 2 * DIFF_QK_WIDTH]
    zv = z[..., FOURIER_WIDTH + 2 * DIFF_QK_WIDTH:]
    f = fourier_mix(zf)
    q = zq.reshape(b, s, DIFF_HEADS, 2, DIFF_QK_DIM)
    k = zk.reshape(b, s, DIFF_HEADS, 2, DIFF_QK_DIM)
    v = zv.reshape(b, s, DIFF_HEADS, DIFF_V_DIM)
    lf = lambdas.astype(jnp.float32)
    lam = jnp.exp(jnp.sum(lf[0] * lf[1])) - jnp.exp(jnp.sum(lf[2] * lf[3])) + lam_init
    a = diff_attention(q, k, v, rel_table, lam, lam_init, subln_g)
    return jnp.concatenate([f, a], axis=-1) @ w_out


def odd_mixer(h, w_uv, v_norm_g, w_s, b_s, w_out):
    b, s, _ = h.shape
    z = jax.nn.gelu(h @ w_uv, approximate=False)
    u = z[..., :SGU_WIDTH]
    v = rms_norm(z[..., SGU_WIDTH:], v_norm_g)
    nc = s // SGU_CHUNK
    vc = v.reshape(b, nc, SGU_CHUNK, SGU_GROUPS, SGU_GROUP_DIM)
    sv = jnp.einsum('gpq,bnqgc->bnpgc', w_s, vc) + jnp.transpose(b_s)[None, None, :, :, None]
    y = u * sv.reshape(b, s, SGU_WIDTH)
    return y @ w_out


def swiglu(h, w1, w3, w2):
    return (jax.nn.silu(h @ w1) * (h @ w3)) @ w2


def setup_inputs(seed: int = 0) -> dict:
    key = jax.random.key(seed)
    ks = jax.random.split(key, 20)
    f32 = jnp.float32
    nrm = lambda k, shape, scale: jax.random.normal(k, shape, f32) * scale
    gain = lambda k, shape: 1.0 + 0.02 * jax.random.normal(k, shape, f32)
    return {
        'x': jax.random.normal(ks[0], (BATCH, SEQ, D_MODEL), f32),
        'rel_bias_table': nrm(ks[1], (REL_BUCKETS, DIFF_HEADS), 0.5),
        'norm_mix_g': gain(ks[2], (DEPTH, D_MODEL)),
        'norm_ffn_g': gain(ks[3], (DEPTH, D_MODEL)),
        'even_w_in': nrm(ks[4], (N_EVEN, D_MODEL, EVEN_IN_WIDTH), D_MODEL ** -0.5),
        'even_w_out': nrm(ks[5], (N_EVEN, EVEN_MIX_WIDTH, D_MODEL), EVEN_MIX_WIDTH ** -0.5),
        'diff_lambda': nrm(ks[6], (N_EVEN, 4, DIFF_QK_DIM), 0.1),
        'diff_subln_g': gain(ks[7], (N_EVEN, DIFF_V_DIM)),
        'odd_w_uv': nrm(ks[8], (N_ODD, D_MODEL, 2 * SGU_WIDTH), D_MODEL ** -0.5),
        'odd_v_norm_g': gain(ks[9], (N_ODD, SGU_WIDTH)),
        'odd_w_s': nrm(ks[10], (N_ODD, SGU_GROUPS, SGU_CHUNK, SGU_CHUNK), SGU_CHUNK ** -0.5),
        'odd_b_s': gain(ks[11], (N_ODD, SGU_GROUPS, SGU_CHUNK)),
        'odd_w_out': nrm(ks[12], (N_ODD, SGU_WIDTH, D_MODEL), SGU_WIDTH ** -0.5),
        'ffn_w1': nrm(ks[13], (DEPTH, D_MODEL, D_FF), D_MODEL ** -0.5),
        'ffn_w3': nrm(ks[14], (DEPTH, D_MODEL, D_FF), D_MODEL ** -0.5),
        'ffn_w2': nrm(ks[15], (DEPTH, D_FF, D_MODEL), D_FF ** -0.5),
        'final_norm_g': gain(ks[16], (D_MODEL,)),
    }


def reference(x, rel_bias_table, norm_mix_g, norm_ffn_g, even_w_in, even_w_out,
              diff_lambda, diff_subln_g, odd_w_uv, odd_v_norm_g, odd_w_s, odd_b_s,
              odd_w_out, ffn_w1, ffn_w3, ffn_w2, final_norm_g):
    h = x
    for i in range(DEPTH):
        hn = rms_norm(h, norm_mix_g[i])
        j = i // 2
        if i % 2 == 0:
            lam_init = 0.8 - 0.6 * math.exp(-0.3 * i)
            m = even_mixer(hn, even_w_in[j], even_w_out[j], rel_bias_table,
                           diff_lambda[j], diff_subln_g[j], lam_init)
        else:
            m = odd_mixer(hn, odd_w_uv[j], odd_v_norm_g[j], odd_w_s[j], odd_b_s[j],
                          odd_w_out[j])
        h = h + m
        hn = rms_norm(h, norm_ffn_g[i])
        h = h + swiglu(hn, ffn_w1[i], ffn_w3[i], ffn_w2[i])
    return rms_norm(h, final_norm_g)
```

```python
import math
import numpy as np
import ml_dtypes
from contextlib import ExitStack
import concourse.bass as bass
import concourse.mybir as mybir
from concourse.bass_utils import run_bass_kernel_spmd

F32 = mybir.dt.float32
BF16 = mybir.dt.bfloat16
I32 = mybir.dt.int32
AF = mybir.ActivationFunctionType
ALU = mybir.AluOpType

D = 1024
S = 4096
NB = 4
DFF = 2816
NJ = DFF // 128
EPS = 1e-6
LAM_INIT0 = 0.8 - 0.6 * math.exp(-0.3 * 0)
KIB = 1024


class Buf:
    __slots__ = ("name", "w", "r", "sem", "cnt", "key")

    def __init__(self, name):
        self.name = name
        self.w = None
        self.r = {}
        self.sem = None
        self.cnt = 0
        self.key = None


class KB:
    def __init__(self, nc, es):
        self.nc = nc
        self.es = es
        self.eng = dict(pe=nc.tensor, act=nc.scalar, dve=nc.vector, pool=nc.gpsimd, sp=nc.sync)
        self.esem = {k: es.enter_context(nc.semaphore("e_" + k)) for k in self.eng}
        self.ecnt = {k: 0 for k in self.eng}
        self.seen = {k: {} for k in self.eng}
        self.dbufs = []
        self.load = dict(act=0.0, dve=0.0)

    def _wait(self, e, evs):
        best = {}
        for ev in evs:
            if ev is None:
                continue
            sem, val, key = ev
            if self.seen[e].get(key, 0) >= val:
                continue
            if key not in best or best[key][1] < val:
                best[key] = ev
        for key, (sem, val, _) in best.items():
            self.eng[e].wait_ge(sem, val)
            self.seen[e][key] = val

    @staticmethod
    def _deps(reads, writes):
        evs = []
        for b in reads:
            if b.w is not None:
                evs.append(b.w)
        for b in writes:
            if b.w is not None:
                evs.append(b.w)
            evs.extend(b.r.values())
        return evs

    @staticmethod
    def _commit(ev, reads, writes):
        for b in reads:
            b.r[ev[2]] = ev
        for b in writes:
            b.w = ev
            b.r = {}

    def acquire(self, e, reads=(), writes=()):
        self._wait(e, self._deps(reads, writes))

    def _signal(self, e, inst):
        self.ecnt[e] += 1
        inst.then_inc(self.esem[e], 1)
        return (self.esem[e], self.ecnt[e], e)

    def op(self, e, fn, reads=(), writes=()):
        self._wait(e, self._deps(reads, writes))
        ev = self._signal(e, fn())
        self._commit(ev, reads, writes)
        return ev

    def group(self, e, fns, reads=(), writes=(), acquire=True):
        if acquire:
            self._wait(e, self._deps(reads, writes))
        inst = None
        for fn in fns:
            inst = fn()
        ev = self._signal(e, inst)
        self._commit(ev, reads, writes)
        return ev

    def dma(self, q, out, in_, reads=(), writes=(), sembuf=None, waw=True):
        deps = self._deps(reads, writes)
        if not waw:
            ws = {id(b.w) for b in writes if b.w is not None}
            deps = [e for e in deps if id(e) not in ws]
        self._wait(q, deps)
        sb = sembuf if sembuf is not None else (writes[0] if writes else reads[0])
        if sb.sem is None:
            sb.key = "d%d" % len(self.dbufs)
            sb.sem = self.es.enter_context(self.nc.semaphore("s_" + sb.key))
            self.dbufs.append(sb)
        sb.cnt += 16
        self.eng[q].dma_start(out=out, in_=in_).then_inc(sb.sem, 16)
        ev = (sb.sem, sb.cnt, sb.key)
        self._commit(ev, reads, writes)
        return ev

    def barrier(self):
        evs = [(self.esem[e], self.ecnt[e], e) for e in self.eng if self.ecnt[e] > 0]
        evs += [(b.sem, b.cnt, b.key) for b in self.dbufs if b.cnt > 0]
        for e in self.eng:
            self._wait(e, evs)

    def pick(self, cost):
        e = "act" if self.load["act"] <= self.load["dve"] else "dve"
        self.load[e] += cost
        return e


def build(stage=99, debug=False):
    nc = bass.Bass("TRN2", target_bir_lowering=False)

    def din(name, shape, dt=F32):
        return nc.dram_tensor(name, list(shape), dt, kind="ExternalInput").ap()

    x = din("x", [S, D])
    w_in_d = din("w_in", [D, 2048])
    w_out0_d = din("w_out0", [D, D])
    w_uv_d = din("w_uv", [D, 2048])
    w_out1_d = din("w_out1", [D, D])
    w1_d = [din("w1_%d" % l, [D, DFF]) for l in range(2)]
    w3_d = [din("w3_%d" % l, [D, DFF]) for l in range(2)]
    w2_d = [din("w2_%d" % l, [DFF, D]) for l in range(2)]
    gvecs_d = din("gvecs", [6, D])
    lam_d = din("lam", [1, 256])
    subg_d = din("subg", [1, 128])
    dftm_d = din("dftm", [4, 2048, 2, 512], BF16)
    cdft_d = din("cdft", [128, 2, 128], BF16)
    wown_d = din("wown", [128, 4, 1152])
    woth_d = din("woth", [128, 4, 2, 512])
    bconst_d = din("bconst", [128, 12])
    wsT_d = din("wsT", [128, 8, 128])
    bs_d = din("bs", [1, 1024])
    out_d = nc.dram_tensor("out", [2048, D], F32, kind="ExternalOutput").ap()
    dbg = {}
    if debug:
        for nm, shp, dt in [("d_kt", [128, 4, S], BF16), ("d_qt", [128, 4, 2048], BF16),
                            ("d_v", [128, 32, 4, 132], BF16), ("d_z", [128, 32, 512], BF16),
                            ("d_ft", [128, 4, 2048], BF16), ("d_a", [128, 16, 512], BF16)]:
            dbg[nm] = nc.dram_tensor(nm, shp, dt, kind="ExternalOutput").ap()

    es = ExitStack()
    with es:
        arena = es.enter_context(nc.sbuf_tensor("arena", [128, 207 * 512], BF16))
        psum = es.enter_context(nc.psum_tensor("psum", [128, 4096], F32))
        k = KB(nc, es)

        def view(off_kib, shape, dt):
            esz = 2 if dt == BF16 else 4
            n = 1
            for s_ in shape[1:]:
                n *= s_
            o = int(off_kib * 512)
            a = arena[:, o:o + n * esz // 2]
            if dt == F32:
                a = a.bitcast(F32)
            if len(shape) == 3:
                a = a.rearrange("p (a b) -> p a b", a=shape[1])
            elif len(shape) == 4:
                a = a.rearrange("p (a b c) -> p a b c", a=shape[1], b=shape[2])
            return a

        def bank(i, n=512):
            return psum[:, i * 512:i * 512 + n]

        def bank_bf(i):
            return psum[:, i * 512:(i + 1) * 512].bitcast(BF16)

        bankB = [Buf("bank%d" % i) for i in range(8)]
        bank_rr = [0]

        def next_bank():
            i = bank_rr[0] % 8
            bank_rr[0] += 1
            return i

        def evac(out, in_, reads, writes, scale=None, n=512, eng=None):
            e = eng or k.pick(0.25 + n / 1000.0)
            if e == "act":
                if scale is None:
                    return k.op("act", lambda: nc.scalar.copy(out=out, in_=in_), reads, writes)
                return k.op("act", lambda: nc.scalar.activation(out=out, in_=in_, func=AF.Copy, scale=float(scale)), reads, writes)
            if scale is None:
                return k.op("dve", lambda: nc.vector.tensor_copy(out=out, in_=in_), reads, writes)
            return k.op("dve", lambda: nc.vector.tensor_scalar(out=out, in0=in_, scalar1=float(scale), scalar2=None, op0=ALU.mult), reads, writes)

        ident = view(200, [128, 128], BF16)
        identf = view(200.25, [128, 128], F32)
        junk = view(201, [128, 1024], BF16)
        junkD = view(206, [128, 128], BF16)
        junkAB = Buf("junkA")
        junkDB = Buf("junkD")
        small = view(203, [128, 256], F32)
        gsub = view(204, [128, 128], F32)
        lam_sb = view(204.5, [128, 256], F32)
        bconst = view(205.5, [128, 12], F32)
        identB = Buf("ident")
        smallB = Buf("small")
        gsubB = Buf("gsub")
        lamB = Buf("lam")
        bconstB = Buf("bconst")
        ss_ap = small[:, 0:8]
        rt_ap = small[:, 8:16]
        rstd_ap = small[:, 16:24]
        nlam = small[:, 24:25]
        lsum = small[:, 26:28]
        lexp = small[:, 28:30]
        ltmp = small[:, 30:31]

        k.op("pool", lambda: nc.gpsimd.memset(identf, 0.0), writes=[identB])
        k.op("pool", lambda: nc.gpsimd.affine_select(out=identf, in_=identf, compare_op=ALU.not_equal, fill=1.0,
                                                      base=0, pattern=[[-1, 128]], channel_multiplier=1),
             reads=[identB], writes=[identB])
        k.op("dve", lambda: nc.vector.tensor_copy(out=ident, in_=identf), reads=[identB], writes=[identB])
        k.dma("sp", lam_sb, lam_d.partition_broadcast(128), writes=[lamB])
        k.dma("sp", gsub, subg_d.partition_broadcast(128), writes=[gsubB])
        k.dma("sp", bconst, bconst_d, writes=[bconstB])
        k.op("dve", lambda: nc.vector.scalar_tensor_tensor(out=junkD[:, 0:64], in0=lam_sb[:, 0:64], scalar=1.0, in1=lam_sb[:, 64:128],
                                                            op0=ALU.mult, op1=ALU.mult, accum_out=lsum[:, 0:1]), reads=[lamB], writes=[smallB, junkDB])
        k.op("dve", lambda: nc.vector.scalar_tensor_tensor(out=junkD[:, 64:128], in0=lam_sb[:, 128:192], scalar=1.0, in1=lam_sb[:, 192:256],
                                                            op0=ALU.mult, op1=ALU.mult, accum_out=lsum[:, 1:2]), reads=[lamB, smallB], writes=[smallB, junkDB])
        k.op("act", lambda: nc.scalar.activation(out=lexp, in_=lsum, func=AF.Exp), reads=[smallB], writes=[smallB])
        k.op("dve", lambda: nc.vector.tensor_tensor(out=ltmp, in0=lexp[:, 1:2], in1=lexp[:, 0:1], op=ALU.subtract), reads=[smallB], writes=[smallB])
        k.op("dve", lambda: nc.vector.tensor_scalar(out=nlam, in0=ltmp, scalar1=-LAM_INIT0, scalar2=None, op0=ALU.add), reads=[smallB], writes=[smallB])
        k.op("dve", lambda: nc.vector.tensor_scalar(out=gsub, in0=gsub, scalar1=1.0 - LAM_INIT0, scalar2=None, op0=ALU.mult), reads=[gsubB], writes=[gsubB])

        NSET = 2
        statB = [Buf("stat%d" % i) for i in range(NSET + 4)]
        stat_rr = [0]

        def stat_set(dedicated=None):
            if dedicated is None:
                i = stat_rr[0] % NSET
                stat_rr[0] += 1
            else:
                i = NSET + dedicated
            b0 = 64 + 32 * i
            return (small[:, b0:b0 + 8], small[:, b0 + 8:b0 + 16], small[:, b0 + 16:b0 + 24], small[:, b0 + 24:b0 + 32], statB[i])

        def sumsq(st, col, src, srcB):
            k.op("act", lambda: nc.scalar.activation(out=junk, in_=src, func=AF.Square, accum_out=st[0][:, col:col + 1]), reads=srcB, writes=[st[4], junkAB])

        def newton(st, n, inv_d):
            SS, X, Y, T, sB = st
            SS, X, Y, T = SS[:, 0:n], X[:, 0:n], Y[:, 0:n], T[:, 0:n]
            Xi, Yi = X.bitcast(I32), Y.bitcast(I32)
            d = lambda fn: k.op("dve", fn, reads=[sB], writes=[sB])
            d(lambda: nc.vector.tensor_scalar(out=X, in0=SS, scalar1=float(inv_d), scalar2=EPS, op0=ALU.mult, op1=ALU.add))
            d(lambda: nc.vector.tensor_scalar(out=Yi, in0=Xi, scalar1=1, scalar2=None, op0=ALU.arith_shift_right))
            d(lambda: nc.vector.tensor_scalar(out=Yi, in0=Yi, scalar1=-1.0, scalar2=float(0x5f3759df), op0=ALU.mult, op1=ALU.add))
            for _ in range(3):
                d(lambda: nc.vector.tensor_tensor(out=T, in0=Y, in1=Y, op=ALU.mult))
                d(lambda: nc.vector.tensor_tensor(out=T, in0=T, in1=X, op=ALU.mult))
                d(lambda: nc.vector.tensor_scalar(out=T, in0=T, scalar1=-0.5, scalar2=1.5, op0=ALU.mult, op1=ALU.add))
                d(lambda: nc.vector.tensor_tensor(out=Y, in0=Y, in1=T, op=ALU.mult))
            return Y

        def rms_batch(items, gv, gvB):
            st = stat_set()
            for i, (src_, srcB_, _, _) in enumerate(items):
                sumsq(st, i, src_, srcB_)
            Y = newton(st, len(items), 1.0 / D)
            for i, (src_, srcB_, dst_, dstB_) in enumerate(items):
                k.op("dve", lambda: nc.vector.scalar_tensor_tensor(out=dst_, in0=src_, scalar=Y[:, i:i + 1], in1=gv, op0=ALU.mult, op1=ALU.mult),
                     reads=list(srcB_) + [st[4], gvB], writes=dstB_)

        nt_ctr = [0]

        def nt_stats(tiles):
            st = stat_set(dedicated=2 + nt_ctr[0] % 2)
            nt_ctr[0] += 1
            for i, tile in enumerate(tiles):
                sumsq(st, i, hres[:, tile, :], hB[tile])
            Y = newton(st, len(tiles), 1.0 / D)
            return st, Y

        def nt_apply(stY, i, tile, gv, gvB, slot, slotB):
            st, Y = stY
            k.op("dve", lambda: nc.vector.scalar_tensor_tensor(out=slot, in0=hres[:, tile, :], scalar=Y[:, i:i + 1], in1=gv, op0=ALU.mult, op1=ALU.mult),
                 reads=list(hB[tile]) + [st[4], gvB], writes=[slotB])

        def nt_tr(slot, slotB, t, hnT_, hnTB_):
            bi = next_bank()
            bb = bank_bf(bi)
            k.group("pe", [(lambda kk=kk: nc.tensor.transpose(out=bb[:, kk * 128:(kk + 1) * 128], in_=slot[:, kk * 128:(kk + 1) * 128], identity=ident))
                           for kk in range(8)], reads=[slotB, identB], writes=[bankB[bi]])
            evac(hnT_[:, :, t * 128:(t + 1) * 128], bb.rearrange("p (a b) -> p a b", a=8), [bankB[bi]], [hnTB_[t]], n=1024)

        def nt_block_now(tiles, gv, gvB, slots, slotsB, hnT_, hnTB_, stY=None):
            if stY is None:
                stY = nt_stats(tiles)
            ns = len(slots)
            warm_pe(16)
            for i, tile in enumerate(tiles):
                nt_apply(stY, i, tile, gv, gvB, slots[i % ns], slotsB[i % ns])
                if i >= 1:
                    nt_tr(slots[(i - 1) % ns], slotsB[(i - 1) % ns], i - 1, hnT_, hnTB_)
                    warm_pe(4)
            nt_tr(slots[(len(tiles) - 1) % ns], slotsB[(len(tiles) - 1) % ns], len(tiles) - 1, hnT_, hnTB_)

        eps_t = view(205.75, [128, 2], F32)
        epsB = Buf("eps")
        k.op("dve", lambda: nc.vector.memset(eps_t, EPS), writes=[epsB])
        EPS_AP = eps_t[:, 0:1]

        warm = view(206.25, [128, 256], BF16)
        warmB = Buf("warm")
        k.op("dve", lambda: nc.vector.memset(warm, 0.0), writes=[warmB])

        def warm_pe(n):
            bi = next_bank()
            k.group("pe", [(lambda: nc.tensor.matmul(bank(bi, 256), lhsT=ident, rhs=warm, start=True, stop=True)) for _ in range(n)],
                    reads=[identB, warmB], writes=[bankB[bi]])

        w_in = view(0, [128, 8, 2048], BF16)
        xt = [view(o_, [128, 2, 1024], F32) for o_ in (32, 40, 184)]
        hn = [view(o_, [128, 2, 1024], BF16) for o_ in (48, 52, 192)]
        hnT = [view(56 + 4 * i, [128, 8, 256], BF16) for i in range(2)]
        KT = view(64, [128, 4, S], BF16)
        Vaug = view(96, [128, 32, 4, 132], BF16)
        QT = view(130, [128, 4, 2048], BF16)
        Z = view(146, [128, 32, 512], BF16)
        gvec = view(178, [128, 1024], F32)
        fT = view(184, [128, 4, 2048], BF16)

        w_inB = [Buf("w_in%d" % i) for i in range(4)]
        xtB = [Buf("xt%d" % i) for i in range(3)]
        hnB = [[Buf("hn") for _ in range(2)] for _ in range(3)]
        hnTB = [[Buf("hnT") for _ in range(2)] for _ in range(2)]
        KTB = [[Buf("KT") for _ in range(16)] for _ in range(4)]
        QTB = [[Buf("QT") for _ in range(8)] for _ in range(4)]
        VB = [Buf("V") for _ in range(32)]
        ZB = [Buf("Z") for _ in range(32)]
        gvecB = Buf("gvec")
        onesB = Buf("ones")

        k.dma("sp", gvec, gvecs_d[0:1, :].partition_broadcast(128), writes=[gvecB])
        for cb in (2, 1, 0, 3):
            k.dma("pool", w_in[:, :, cb * 512:(cb + 1) * 512],
                  w_in_d[:, cb * 512:(cb + 1) * 512].rearrange("(k p) c -> p k c", p=128), writes=[w_inB[cb]])
            if cb == 2:
                k.acquire("pool", reads=[w_inB[2]])
        k.op("dve", lambda: nc.vector.memset(Vaug[:, :, :, 128:129], 1.0), writes=[onesB])

        def load_x(g):
            k.dma("sp", xt[g % 3], x[g * 256:(g + 1) * 256, :].rearrange("(t p) d -> p t d", p=128), writes=[xtB[g % 3]])

        def normA(g):
            s3 = g % 3
            rms_batch([(xt[s3][:, t, :], [xtB[s3]], hn[s3][:, t, :], [hnB[s3][t]]) for t in range(2)], gvec, gvecB)

        def transA(g):
            sl = g % 2
            s3 = g % 3
            for t in range(2):
                bi = next_bank()
                bb = bank_bf(bi)
                k.group("pe", [(lambda kk=kk: nc.tensor.transpose(out=bb[:, kk * 128:(kk + 1) * 128], in_=hn[s3][:, t, kk * 128:(kk + 1) * 128], identity=ident))
                               for kk in range(8)], reads=[hnB[s3][t], identB], writes=[bankB[bi]])
                evac(hnT[sl][:, :, t * 128:(t + 1) * 128], bb.rearrange("p (a b) -> p a b", a=8), [bankB[bi]], [hnTB[sl][t]], n=1024)

        load_x(0)
        k.acquire("sp", reads=[xtB[0]])
        load_x(1)
        load_x(2)
        normA(0)
        normA(1)
        transA(0)
        for g in range(16):
            own = g < 8
            sl = g % 2
            if g + 3 < 16:
                load_x(g + 3)
            if g + 2 < 16:
                normA(g + 2)
            hT = hnT[sl]
            hTB = hnTB[sl]
            for which in ("k", "q"):
                if which == "q" and not own:
                    continue
                for h in range(4):
                    c0 = (1024 if which == "k" else 512) + h * 128
                    bi = next_bank()
                    bk = bank(bi, 256)
                    k.group("pe", [(lambda kk=kk: nc.tensor.matmul(bk, lhsT=w_in[:, kk, c0:c0 + 128], rhs=hT[:, kk, :], start=(kk == 0), stop=(kk == 7)))
                                   for kk in range(8)], reads=[w_inB[2 if which == "k" else 1]] + hTB, writes=[bankB[bi]])
                    if which == "k":
                        evac(KT[:, h, g * 256:(g + 1) * 256], bk, [bankB[bi]], [KTB[h][g]], n=256)
                    else:
                        evac(QT[:, h, g * 256:(g + 1) * 256], bk, [bankB[bi]], [QTB[h][g]], scale=0.125, n=256)
            if g + 1 < 16:
                transA(g + 1)
            for t in range(2):
                tile = g * 2 + t
                for which in ("z", "v"):
                    c0 = 0 if which == "z" else 1536
                    bi = next_bank()
                    bk = bank(bi)
                    k.group("pe", [(lambda kk=kk: nc.tensor.matmul(bk, lhsT=hT[:, kk, t * 128:(t + 1) * 128], rhs=w_in[:, kk, c0:c0 + 512], start=(kk == 0), stop=(kk == 7)))
                                   for kk in range(8)], reads=[w_inB[0 if which == "z" else 3], hTB[t]], writes=[bankB[bi]])
                    if which == "z":
                        evac(Z[:, tile, :], bk, [bankB[bi]], [ZB[tile]])
                    else:
                        evac(Vaug[:, tile, :, 0:128], bk.rearrange("p (h e) -> p h e", h=4), [bankB[bi]], [VB[tile]])
        k.barrier()
        if debug:
            k.dma("sp", dbg["d_kt"], KT, reads=[KTB[0][0]], sembuf=Buf("dbg"))
            k.dma("sp", dbg["d_qt"], QT, reads=[KTB[0][0]], sembuf=Buf("dbg"))
            k.dma("sp", dbg["d_v"], Vaug, reads=[KTB[0][0]], sembuf=Buf("dbg"))
            k.dma("sp", dbg["d_z"], Z, reads=[KTB[0][0]], sembuf=Buf("dbg"))
            k.barrier()
        if stage <= 1:
            return nc

        ring = [view(2 * i, [128, 2, 512], BF16) for i in range(16)]
        UV = view(56, [128, 8, 512], BF16)
        cdft = view(182, [128, 2, 128], BF16)
        ringB = [Buf("ring%d" % i) for i in range(16)]
        UVB = [Buf("UV%d" % i) for i in range(8)]
        cdftB = Buf("cdft")
        fTB = [[Buf("fT") for _ in range(4)] for _ in range(4)]
        k.dma("sp", cdft, cdft_d, writes=[cdftB])

        Zm = view(32, [128, 16, 512], BF16)
        Zpm = view(48, [128, 8, 512], BF16)
        ZmB = [Buf("Zm%d" % i) for i in range(16)]
        ZpmB = [Buf("Zpm%d" % i) for i in range(8)]
        for a in range(16):
            k.op("dve", lambda: nc.vector.tensor_tensor(out=Zm[:, a, :], in0=Z[:, a, :], in1=Z[:, a + 16, :], op=ALU.subtract),
                 reads=[ZB[a], ZB[a + 16]], writes=[ZmB[a]])
            k.op("dve", lambda: nc.vector.tensor_tensor(out=Z[:, a, :], in0=Z[:, a, :], in1=Z[:, a + 16, :], op=ALU.add),
                 reads=[ZB[a + 16]], writes=[ZB[a]])
        for a in range(8):
            k.op("dve", lambda: nc.vector.tensor_tensor(out=Zpm[:, a, :], in0=Z[:, a, :], in1=Z[:, a + 8, :], op=ALU.subtract),
                 reads=[ZB[a], ZB[a + 8]], writes=[ZpmB[a]])
            k.op("dve", lambda: nc.vector.tensor_tensor(out=Z[:, a, :], in0=Z[:, a, :], in1=Z[:, a + 8, :], op=ALU.add),
                 reads=[ZB[a + 8]], writes=[ZB[a]])

        chunk_src = {0: (Z, ZB, 8, slice(0, 2048, 4)), 1: (Zpm, ZpmB, 8, slice(2, 2048, 4)),
                     2: (Zm, ZmB, 16, slice(1, 1024, 2)), 3: (Zm, ZmB, 16, slice(1025, 2048, 2))}
        corder = (2, 3, 0, 1)

        def load_piece(c_, a_):
            k.dma("sp", ring[a_], dftm_d[c_, a_ * 128:(a_ + 1) * 128, :, :], writes=[ringB[a_]])

        for a_ in range(chunk_src[corder[0]][2]):
            load_piece(corder[0], a_)
        hidx = 0
        pending = []
        for ci, c in enumerate(corder):
            zsrc, zsrcB, na, osl = chunk_src[c]
            nxt = corder[ci + 1] if ci + 1 < 4 else None
            for half in range(2):
                b0 = 4 * (hidx % 2)
                hidx += 1
                gs = (2 * half, 2 * half + 1)
                hb = [bankB[b0 + j] for j in range(4)]
                k.acquire("pe", writes=hb)
                ev = None
                for a in range(na):
                    rg = ring[a]
                    fns = []
                    for j, g in enumerate(gs):
                        zl = zsrc[:, a, g * 128:(g + 1) * 128]
                        fns.append(lambda j=j, zl=zl, rg=rg: nc.tensor.matmul(bank(b0 + j), lhsT=zl, rhs=rg[:, 0, :], start=(a == 0), stop=(a == na - 1)))
                        fns.append(lambda j=j, zl=zl, rg=rg: nc.tensor.matmul(bank(b0 + 2 + j), lhsT=zl, rhs=rg[:, 1, :], start=(a == 0), stop=(a == na - 1)))
                    ev = k.group("pe", fns, reads=[ringB[a], zsrcB[a]], writes=[])
                    if half == 1 and nxt is not None and a < chunk_src[nxt][2]:
                        load_piece(nxt, a)
                    if a == 3 and pending:
                        pending.pop(0)()
                for b_ in hb:
                    b_.w = ev
                    b_.r = {}
                for j in range(4):
                    evac(UV[:, b0 + j, :], bank(b0 + j), [bankB[b0 + j]], [UVB[b0 + j]])

                def chan(b0=b0, gs=gs, osl=osl, c=c):
                    for j, g in enumerate(gs):
                        k.group("pe", [lambda j=j: nc.tensor.matmul(bank(b0 + j), lhsT=cdft[:, 0, :], rhs=UV[:, b0 + j, :], start=True, stop=False),
                                       lambda j=j: nc.tensor.matmul(bank(b0 + j), lhsT=cdft[:, 1, :], rhs=UV[:, b0 + 2 + j, :], start=False, stop=True)],
                                reads=[UVB[b0 + j], UVB[b0 + 2 + j], cdftB], writes=[bankB[b0 + j]])
                        evac(fT[:, g, osl], bank(b0 + j), [bankB[b0 + j]], [fTB[g][c]])
                pending.append(chan)
        while pending:
            pending.pop(0)()
        k.barrier()
        if debug:
            k.dma("sp", dbg["d_ft"], fT, reads=[KTB[0][0]], sembuf=Buf("dbg"))
            k.barrier()
        if stage <= 2:
            return nc

        wown = view(0, [128, 4, 1152], F32)
        woth = view(18, [128, 4, 2, 512], F32)
        Pb = [view(34 + 2 * i, [128, 1024], BF16) for i in range(3)]
        Tb = [view(40 + 4 * i, [128, 2, 512], F32) for i in range(2)]
        osb = [view(48 + 2 * i, [128, 4, 128], F32) for i in range(2)]
        o1t = [view(52 + 0.5 * i, [128, 128], F32) for i in range(2)]
        Osb = [view(53 + 4.25 * i, [128, 4, 264], F32) for i in range(2)]
        OsbB = [Buf("Osb%d" % i) for i in range(2)]
        a_tm = view(146, [128, 16, 512], BF16)
        aT = view(162, [128, 4, 2048], BF16)
        w_out0 = view(64, [128, 8, 1024], BF16)
        wownBh = [Buf("wown%d" % i) for i in range(4)]
        wothBh = [Buf("woth%d" % i) for i in range(4)]
        PB = [Buf("P%d" % i) for i in range(3)]
        TB = [[Buf("T%d_%d" % (i, m)) for m in range(2)] for i in range(2)]
        osB = [[Buf("os") for _ in range(4)] for _ in range(2)]
        o1B = [Buf("o1t%d" % i) for i in range(2)]
        aB = [[Buf("a") for _ in range(4)] for _ in range(16)]
        aTB = [Buf("aT%d" % i) for i in range(16)]
        w_out0B = Buf("w_out0")
        LB = [Buf("L0"), Buf("L1")]
        OB = [Buf("O%d" % i) for i in range(4)]
        rsB = Buf("rs")
        hsB = Buf("hs")
        rs = small[:, 32:40].rearrange("p (b j) -> p b j", b=4)
        hss = small[:, 40:44]
        hln = small[:, 44:48]
        hrs = small[:, 48:52]
        for h_ in range(4):
            k.dma("sp", wown[:, h_, :], wown_d[:, h_, :], writes=[wownBh[h_]])
            k.dma("sp", woth[:, h_, :, :], woth_d[:, h_, :, :], writes=[wothBh[h_]])
        Obank = psum[:, 2048:4096].rearrange("p (b c) -> p b c", b=4)
        Lq = [psum[:, 0:1024], psum[:, 1024:2048]]
        iters = [(h_, qc_) for h_ in range(4) for qc_ in range(4)]
        pcount = [0]
        pb_of = {}

        def order_of(qc_):
            s_ = {0: 5, 1: 9, 2: 13, 3: 17}[qc_]
            return [(s_ + p_) % 32 for p_ in range(32)]
        orders = [order_of(qc_) for (_, qc_) in iters]
        Lidx = {}

        def qk(it, pos):
            h, qc = iters[it]
            kt = orders[it][pos]
            i = (it * 32 + pos) % 2
            Lidx[(it, pos)] = i
            ks = slice(kt * 128, (kt + 1) * 128)
            qs = slice(qc * 512, (qc + 1) * 512)
            k.group("pe", [lambda: nc.tensor.matmul(Lq[i][:, 0:512], lhsT=KT[0:64, h, ks], rhs=QT[0:64, h, qs], start=True, stop=True),
                           lambda: nc.tensor.matmul(Lq[i][:, 512:1024], lhsT=KT[64:128, h, ks], rhs=QT[64:128, h, qs], start=True, stop=True)],
                    reads=[KTB[h][kt // 2]] + [QTB[h][2 * qc], QTB[h][2 * qc + 1]], writes=[LB[i]])

        def softmax_tile(it, pos):
            h, qc = iters[it]
            kt = orders[it][pos]
            pb = pcount[0] % 3
            pcount[0] += 1
            pb_of[(it, pos)] = pb
            i = Lidx[(it, pos)]
            argB = None
            if kt < 16:
                dl = kt * 128 - qc * 512
                if -255 < dl < 639:
                    mode, arg, argB = "near", wown[:, h, 512 - dl:1024 - dl], wownBh[h]
                else:
                    col = 3 * h + (1 if dl >= 639 else 0)
                    mode, arg = "far", bconst[:, col:col + 1]
            elif (kt, qc) == (16, 3):
                mode, arg, argB = "near", woth[:, h, 0, :], wothBh[h]
            elif (kt, qc) == (31, 0):
                mode, arg, argB = "near", woth[:, h, 1, :], wothBh[h]
            else:
                mode, arg = "far", bconst[:, 3 * h + 2:3 * h + 3]
            if mode == "far":
                k.op("act", lambda: nc.scalar.activation(out=Pb[pb], in_=Lq[i], func=AF.Exp, bias=arg, scale=1.0),
                     reads=[LB[i], bconstB], writes=[PB[pb]])
            else:
                tb = pos % 2
                for m in range(2):
                    ms = slice(m * 512, (m + 1) * 512)
                    k.op("dve", lambda: nc.vector.tensor_tensor(out=Tb[tb][:, m, :], in0=Lq[i][:, ms], in1=arg, op=ALU.add),
                         reads=[LB[i], argB], writes=[TB[tb][m]])
                k.op("act", lambda: nc.scalar.activation(out=Pb[pb], in_=Tb[tb].rearrange("p m q -> p (m q)"), func=AF.Exp),
                     reads=TB[tb], writes=[PB[pb]])

        def av(it, pos):
            h, qc = iters[it]
            kt = orders[it][pos]
            pb = pb_of[(it, pos)]
            fns = []
            for m in range(2):
                for qt in range(4):
                    bnk = 2 * m + qt // 2
                    j = qt % 2
                    fns.append(lambda m=m, qt=qt, bnk=bnk, j=j: nc.tensor.matmul(
                        Obank[:, bnk, j * 129:(j + 1) * 129], lhsT=Pb[pb][:, m * 512 + qt * 128:m * 512 + (qt + 1) * 128],
                        rhs=Vaug[:, kt, h, 0:129], start=(pos == 0 and j == 0), stop=(pos == 31), skip_group_check=True))
            if pos == 0:
                k.acquire("pe", writes=OB)
            ev_ = k.group("pe", fns, reads=[PB[pb], VB[kt], onesB], writes=[])
            if pos == 31:
                for b_ in OB:
                    b_.w = ev_
                    b_.r = {}

        epi_st = {}

        def epi0(it):
            e = it % 2
            for b_ in range(4):
                k.op("dve", lambda: nc.vector.tensor_copy(out=Osb[e][:, b_, 0:258], in_=Obank[:, b_, 0:258]), reads=[OB[b_]], writes=[OsbB[e]])

        def epi1(it):
            e = it % 2
            Ob = Osb[e]
            k.op("dve", lambda: nc.vector.reciprocal(out=rs, in_=Ob[:, :, 128:258:129]), reads=[OsbB[e]], writes=[rsB])
            k.op("dve", lambda: nc.vector.tensor_scalar(out=rs[:, 2:4, :], in0=rs[:, 2:4, :], scalar1=nlam, scalar2=None, op0=ALU.mult),
                 reads=[rsB, smallB], writes=[rsB])
            for qt in range(4):
                pr, j = qt // 2, qt % 2
                k.op("dve", lambda: nc.vector.tensor_scalar(out=o1t[qt % 2], in0=Ob[:, pr, j * 129:j * 129 + 128], scalar1=rs[:, pr, j:j + 1],
                                                            scalar2=None, op0=ALU.mult), reads=[OsbB[e], rsB], writes=[o1B[qt % 2]])
                k.op("dve", lambda: nc.vector.scalar_tensor_tensor(out=osb[e][:, qt, :], in0=Ob[:, 2 + pr, j * 129:j * 129 + 128],
                                                                    scalar=rs[:, 2 + pr, j:j + 1], in1=o1t[qt % 2], op0=ALU.mult, op1=ALU.add),
                     reads=[OsbB[e], rsB, o1B[qt % 2]], writes=[osB[e][qt]])
            st_h = stat_set()
            epi_st[it] = st_h
            for qt in range(4):
                k.op("dve", lambda: nc.vector.scalar_tensor_tensor(out=junkD, in0=osb[e][:, qt, :], scalar=1.0, in1=osb[e][:, qt, :],
                                                                    op0=ALU.mult, op1=ALU.mult, accum_out=st_h[0][:, qt:qt + 1]),
                     reads=[osB[e][qt]], writes=[st_h[4], junkDB])

        def epi2(it):
            h, qc = iters[it]
            e = it % 2
            st_h = epi_st.pop(it)
            Yh = newton(st_h, 4, 1.0 / 128)
            for qt in range(4):
                k.op("dve", lambda: nc.vector.scalar_tensor_tensor(out=a_tm[:, qc * 4 + qt, h * 128:(h + 1) * 128], in0=osb[e][:, qt, :],
                                                                    scalar=Yh[:, qt:qt + 1], in1=gsub, op0=ALU.mult, op1=ALU.mult),
                     reads=[osB[e][qt], st_h[4], gsubB], writes=[aB[qc * 4 + qt][h]])

        qk(0, 0)
        qk(0, 1)
        for it in range(16):
            for kt in range(32):
                softmax_tile(it, kt)
                if it > 0 and kt == 0:
                    epi0(it - 1)
                if it > 0 and kt == 4:
                    epi1(it - 1)
                if it > 0 and kt == 12:
                    epi2(it - 1)
                if kt + 2 < 32:
                    qk(it, kt + 2)
                elif it + 1 < 16:
                    qk(it + 1, kt + 2 - 32)
                av(it, kt)
            if it == 7:
                k.acquire("pool", writes=KTB[0] + KTB[1])
                k.dma("pool", w_out0, w_out0_d.rearrange("(k p) c -> p k c", p=128), reads=[], writes=[w_out0B])
        epi0(15)
        epi1(15)
        epi2(15)
        warm_pe(60)
        k.barrier()
        for qt in range(16):
            bi = next_bank()
            bb = bank_bf(bi)
            k.group("pe", [(lambda hh=hh: nc.tensor.transpose(out=bb[:, hh * 128:(hh + 1) * 128], in_=a_tm[:, qt, hh * 128:(hh + 1) * 128], identity=ident))
                           for hh in range(4)], reads=aB[qt] + [identB], writes=[bankB[bi]])
            evac(aT[:, :, qt * 128:(qt + 1) * 128], bb[:, 0:512].rearrange("p (a b) -> p a b", a=4), [bankB[bi]], [aTB[qt]])
        if debug:
            k.barrier()
            k.dma("sp", dbg["d_a"], a_tm, reads=[KTB[0][0]], sembuf=Buf("dbg"))
            k.barrier()
        if stage <= 3:
            return nc

        k.barrier()
        hres = view(0, [128, 16, 1024], F32)
        hB = [[Buf("h") for _ in range(2)] for _ in range(16)]
        xr = [view(130 + 4 * i, [128, 1024], F32) for i in range(3)]
        xrB = [Buf("xr%d" % i) for i in range(3)]
        for i in range(2):
            k.dma("sp", xr[i], x[i * 128:(i + 1) * 128, :], writes=[xrB[i]])
        stY_box = {}
        for qt in range(16):
            if qt == 9:
                stY_box["ffn0"] = nt_stats(list(range(0, 8)))
            if qt + 2 < 16:
                k.dma("sp", xr[(qt + 2) % 3], x[(qt + 2) * 128:(qt + 3) * 128, :], writes=[xrB[(qt + 2) % 3]])
            for half in range(2):
                bi = next_bank()
                bk = bank(bi)
                hs_ = slice(half * 512, (half + 1) * 512)
                k.group("pe", [(lambda kk=kk: nc.tensor.matmul(bk, lhsT=(fT[:, kk, qt * 128:(qt + 1) * 128] if kk < 4 else aT[:, kk - 4, qt * 128:(qt + 1) * 128]),
                                                               rhs=w_out0[:, kk, hs_], start=(kk == 0), stop=(kk == 7))) for kk in range(8)],
                        reads=[aTB[qt], w_out0B], writes=[bankB[bi]])
                k.op("dve", lambda: nc.vector.tensor_tensor(out=hres[:, qt, hs_], in0=bk, in1=xr[qt % 3][:, hs_], op=ALU.add),
                     reads=[bankB[bi], xrB[qt % 3]], writes=[hB[qt][half]])
        k.barrier()
        def dump_h():
            stB = Buf("st")
            for qt in range(16):
                k.dma("sp", out_d[qt * 128:(qt + 1) * 128, :], hres[:, qt, :], reads=hB[qt], sembuf=stB)
            k.barrier()

        if stage <= 4:
            dump_h()
            return nc

        def next_pair():
            if bank_rr[0] % 2:
                bank_rr[0] += 1
            i = bank_rr[0] % 8
            bank_rr[0] += 2
            return i

        FFB = {}

        def ffn(l, gidx, final=None, pre_down1=None, stY0=None, pre=None):
            hn2 = [view(o_, [128, 1024], BF16) for o_ in (64, 66, 196, 198)]
            gvF = view(68, [128, 1024], F32)
            hnT_ = view(72, [128, 8, 1024], BF16)
            GT = view(88, [128, NJ, 1024], BF16)
            W2 = view(132, [128, NJ, 1024], BF16)
            w13 = [view(176 + 8 * i, [128, 2, 8, 256], BF16) for i in range(2)]
            sil = [view(192 + 2 * i, [128, 512], F32) for i in range(2)]
            hn2B = [Buf("hn2") for _ in range(4)]
            gvFB = FFB.setdefault("gvF", Buf("gvF"))
            hnTB_ = [Buf("hnT") for _ in range(8)]
            if pre is not None:
                hn2B = pre["hn2B"] + hn2B[2:]
                hnTB_ = pre["hnTB_"]
            GTB = [[Buf("GT") for _ in range(2)] for _ in range(NJ)]
            W2B = [FFB.setdefault("W2", Buf("W2"))]
            w13B = FFB.setdefault("w13", [[Buf("w1_%d" % i), Buf("w3_%d" % i)] for i in range(2)])
            silB = [Buf("sil%d" % i) for i in range(2)]
            if pre is None:
                k.dma("sp", gvF, gvecs_d[gidx:gidx + 1, :].partition_broadcast(128), writes=[gvFB])
            npair = NJ // 2
            pair_ctr = [0]
            w2_next = [0]

            def load_pair():
                i = pair_ctr[0]
                pair_ctr[0] += 1
                jp = i % npair
                sl = i % 2
                cs = slice(jp * 256, (jp + 1) * 256)
                if not (i == 0 and pre is not None):
                    k.dma("pool", w13[sl][:, 0, :, :], w1_d[l][:, cs].rearrange("(k p) c -> p k c", p=128), writes=[w13B[sl][0]])
                    k.dma("pool", w13[sl][:, 1, :, :], w3_d[l][:, cs].rearrange("(k p) c -> p k c", p=128), writes=[w13B[sl][1]])
                for _ in range(2):
                    j = w2_next[0]
                    if j < NJ:
                        w2_next[0] += 1
                        k.dma("pool", W2[:, j, :], w2_d[l][j * 128:(j + 1) * 128, :], writes=[W2B[0]], waw=False)

            load_pair()
            load_pair()
            sctr = [0]

            def up(blk):
                for jp in range(npair):
                    sl = (blk * npair + jp) % 2
                    for jj in range(2):
                        j = 2 * jp + jj
                        for tc in range(2):
                            ts = slice(tc * 512, (tc + 1) * 512)
                            b1 = next_bank()
                            b3 = next_bank()
                            k.group("pe", [(lambda kk=kk: nc.tensor.matmul(bank(b1), lhsT=w13[sl][:, 0, kk, jj * 128:(jj + 1) * 128], rhs=hnT_[:, kk, ts], start=(kk == 0), stop=(kk == 7)))
                                           for kk in range(8)], reads=[w13B[sl][0]] + hnTB_[tc * 4:tc * 4 + 4], writes=[bankB[b1]])
                            k.group("pe", [(lambda kk=kk: nc.tensor.matmul(bank(b3), lhsT=w13[sl][:, 1, kk, jj * 128:(jj + 1) * 128], rhs=hnT_[:, kk, ts], start=(kk == 0), stop=(kk == 7)))
                                           for kk in range(8)], reads=[w13B[sl][1]] + hnTB_[tc * 4:tc * 4 + 4], writes=[bankB[b3]])
                            ss_ = sctr[0] % 2
                            sctr[0] += 1
                            k.op("act", lambda: nc.scalar.activation(out=sil[ss_], in_=bank(b1), func=AF.Silu), reads=[bankB[b1]], writes=[silB[ss_]])
                            k.op("dve", lambda: nc.vector.tensor_tensor(out=GT[:, j, ts], in0=bank(b3), in1=sil[ss_], op=ALU.mult),
                                 reads=[bankB[b3], silB[ss_]], writes=[GTB[j][tc]])
                    if pair_ctr[0] < 2 * npair:
                        load_pair()

            def down(blk, hook=None):
                for t in range(8):
                    tile = blk * 8 + t
                    if hook is not None:
                        hook(t)
                    for half in range(2):
                        bi = next_bank()
                        hs_ = slice(half * 512, (half + 1) * 512)
                        k.group("pe", [(lambda j=j: nc.tensor.matmul(bank(bi), lhsT=GT[:, j, t * 128:(t + 1) * 128], rhs=W2[:, j, hs_], start=(j == 0), stop=(j == NJ - 1)))
                                       for j in range(NJ)], reads=[GTB[j][t // 4] for j in range(NJ)] + W2B, writes=[bankB[bi]])
                        k.op("dve", lambda: nc.vector.tensor_tensor(out=hres[:, tile, hs_], in0=bank(bi), in1=hres[:, tile, hs_], op=ALU.add),
                             reads=[bankB[bi]], writes=[hB[tile][half]])

            if pre is None:
                nt_block_now(list(range(0, 8)), gvF, gvFB, hn2, hn2B, hnT_, hnTB_, stY=stY0)
            stY1 = nt_stats(list(range(8, 16)))

            def hook1(t):
                nt_apply(stY1, t, 8 + t, gvF, gvFB, hn2[t % 4], hn2B[t % 4])
                if t >= 1:
                    nt_tr(hn2[(t - 1) % 4], hn2B[(t - 1) % 4], t - 1, hnT_, hnTB_)

            up(0)
            down(0, hook1)
            nt_tr(hn2[3], hn2B[3], 7, hnT_, hnTB_)
            up(1)
            if pre_down1 is not None:
                pre_down1(hn2B + [gvFB] + hnTB_ + [b_ for pr_ in w13B for b_ in pr_] + silB)
            if final is None:
                down(1)
            else:
                fin = final(hn2B + [gvFB] + hnTB_)
                down(1, fin)
                fin(8)
            k.barrier()

        MX = {}

        def carry_of(old_bufs):
            carry = {}
            for ob in old_bufs:
                for ev in ([ob.w] if ob.w is not None else []) + list(ob.r.values()):
                    if ev[2] not in carry or carry[ev[2]][1] < ev[1]:
                        carry[ev[2]] = ev
            return carry

        def mix_pre(dead):
            carry = carry_of(dead)
            MX["w_uv"] = [view(o_, [128, 8, 512], BF16) for o_ in (64, 72, 80, 176)]
            MX["w_o1"] = view(184, [128, 8, 1024], BF16)
            MX["w_uvB"] = [Buf("w_uv%d" % i) for i in range(4)]
            MX["w_o1B"] = Buf("w_o1")
            for nb in MX["w_uvB"] + [MX["w_o1B"]]:
                nb.r = dict(carry)
            for cb in range(4):
                k.dma("pool", MX["w_uv"][cb], w_uv_d[:, cb * 512:(cb + 1) * 512].rearrange("(k p) c -> p k c", p=128), writes=[MX["w_uvB"][cb]])
            k.dma("pool", MX["w_o1"], w_out1_d.rearrange("(k p) c -> p k c", p=128), writes=[MX["w_o1B"]])
            MX["stY0"] = nt_stats(list(range(0, 8)))

        ffn(0, 1, pre_down1=mix_pre, stY0=stY_box.get("ffn0"))
        if stage <= 5:
            dump_h()
            return nc

        w_uv = MX["w_uv"]
        w_o1 = MX["w_o1"]
        hnT1 = view(88, [128, 8, 1024], BF16)
        uT = view(104, [128, 8, 1024], BF16)
        yT = view(120, [128, 8, 1024], BF16)
        vn = [view(136 + 2 * i, [128, 1024], BF16) for i in range(8)]
        wsTf = view(152, [128, 8, 128], F32)
        wsr = [view(156 + 2 * i, [128, 8, 128], BF16) for i in range(2)]
        bsb = view(160, [128, 8, 128], F32)
        gm1 = view(164, [128, 1024], F32)
        hn2m = [view(168 + 2 * i, [128, 1024], BF16) for i in range(2)]
        svt = view(172, [128, 8, 128], F32)
        gvT = small[:, 56:64]
        w_uvB = MX["w_uvB"]
        w_o1B = MX["w_o1B"]
        wsTB, bsbB, gvTB, gm1B = Buf("wsT"), Buf("bsb"), Buf("gvT"), Buf("gm1")
        hnT1B = [Buf("hnT1") for _ in range(8)]
        uTB = [[Buf("uT") for _ in range(2)] for _ in range(8)]
        yTB = [Buf("yT") for _ in range(8)]
        vnB = [[Buf("vn") for _ in range(2)] for _ in range(8)]
        wsrB = [Buf("wsr") for _ in range(2)]
        hn2mB = [Buf("hn2m") for _ in range(2)]
        svtB = Buf("svt")
        k.dma("sp", gm1, gvecs_d[2:3, :].partition_broadcast(128), writes=[gm1B])
        k.dma("sp", bsb.rearrange("p g c -> p (g c)"), bs_d.partition_broadcast(128), writes=[bsbB])
        k.dma("sp", wsTf, wsT_d, writes=[wsTB])
        with nc.allow_non_contiguous_dma(reason="tiny transposed gain vector"):
            k.dma("sp", gvT, gvecs_d[5:6, :].rearrange("o (g c) -> (o c) g", g=8), writes=[gvTB])

        def mix_O(blk, t):
            tile = blk * 8 + t
            for half in range(2):
                bi = next_bank()
                hs_ = slice(half * 512, (half + 1) * 512)
                k.group("pe", [(lambda kk=kk: nc.tensor.matmul(bank(bi), lhsT=yT[:, kk, t * 128:(t + 1) * 128], rhs=w_o1[:, kk, hs_], start=(kk == 0), stop=(kk == 7)))
                               for kk in range(8)], reads=[yTB[t], w_o1B], writes=[bankB[bi]])
                k.op("dve", lambda: nc.vector.tensor_tensor(out=hres[:, tile, hs_], in0=bank(bi), in1=hres[:, tile, hs_], op=ALU.add),
                     reads=[bankB[bi]], writes=[hB[tile][half]])

        nt_block_now(list(range(0, 8)), gm1, gm1B, hn2m, hn2mB, hnT1, hnT1B, stY=MX["stY0"])
        F1 = {}
        for blk in range(2):
            vst = [None, None]
            vY = [None, None]
            stYn = nt_stats(list(range(8, 16))) if blk == 0 else None
            if blk == 1:
                MX["stY_ffn1"] = nt_stats(list(range(0, 8)))
            Spp = {}

            def S_a(t):
                if blk == 0:
                    nt_apply(stYn, t, 8 + t, gm1, gm1B, hn2m[t % 2], hn2mB[t % 2])
                else:
                    nt_apply(MX["stY_ffn1"], t, t, F1["gvF"], FFB["gvF"], F1["hn2"][t % 2], F1["hn2B"][t % 2])
                st_ = vst[t // 4]
                rcol = vY[t // 4][:, t % 4:t % 4 + 1]
                wr = t % 2
                k.op("pool", lambda: nc.gpsimd.tensor_scalar(out=wsr[wr], in0=wsTf, scalar1=rcol, scalar2=1.0, op0=ALU.mult, op1=ALU.mult),
                     reads=[wsTB, st_[4]], writes=[wsrB[wr]])
                pi = next_pair()
                pp = psum[:, pi * 512:(pi + 2) * 512]
                k.group("pe", [(lambda g=g: nc.tensor.matmul(pp[:, g * 128:(g + 1) * 128], lhsT=vn[t][:, g * 128:(g + 1) * 128], rhs=wsr[wr][:, g, :], start=True, stop=True))
                               for g in range(8)], reads=vnB[t] + [wsrB[wr]], writes=[bankB[pi], bankB[pi + 1]])
                k.acquire("act", reads=[bankB[pi], bankB[pi + 1], gvTB], writes=[svtB])
                ev_ = None
                for g in range(8):
                    ev_ = k.op("act", lambda: nc.scalar.activation(out=svt[:, g, :], in_=pp[:, g * 128:(g + 1) * 128], func=AF.Copy, scale=gvT[:, g:g + 1]))
                for b_ in (bankB[pi], bankB[pi + 1]):
                    b_.r[ev_[2]] = ev_
                svtB.w = ev_
                svtB.r = {}
                k.op("dve", lambda: nc.vector.tensor_tensor(out=yT[:, :, t * 128:(t + 1) * 128], in0=svt, in1=bsb, op=ALU.add),
                     reads=[svtB, bsbB], writes=[yTB[t]])

            def S_b(t):
                k.op("dve", lambda: nc.vector.tensor_tensor(out=yT[:, :, t * 128:(t + 1) * 128], in0=yT[:, :, t * 128:(t + 1) * 128],
                                                            in1=uT[:, :, t * 128:(t + 1) * 128], op=ALU.mult),
                     reads=[uTB[c_][t // 4] for c_ in range(8)], writes=[yTB[t]])

            def U_stage(c):
                for tc in range(2):
                    ts = slice(tc * 512, (tc + 1) * 512)
                    bi = next_bank()
                    k.group("pe", [(lambda kk=kk: nc.tensor.matmul(bank(bi), lhsT=w_uv[c // 4][:, kk, (c % 4) * 128:(c % 4 + 1) * 128], rhs=hnT1[:, kk, ts], start=(kk == 0), stop=(kk == 7)))
                                   for kk in range(8)], reads=[w_uvB[c // 4]] + hnT1B[tc * 4:tc * 4 + 4], writes=[bankB[bi]])
                    k.op("act", lambda: nc.scalar.activation(out=uT[:, c, ts], in_=bank(bi), func=AF.Gelu), reads=[bankB[bi]], writes=[uTB[c][tc]])

            def V_stage(t):
                for half in range(2):
                    bi = next_bank()
                    hs_ = slice(half * 512, (half + 1) * 512)
                    k.group("pe", [(lambda kk=kk: nc.tensor.matmul(bank(bi), lhsT=hnT1[:, kk, t * 128:(t + 1) * 128], rhs=w_uv[2 + half][:, kk, :], start=(kk == 0), stop=(kk == 7)))
                                   for kk in range(8)], reads=[w_uvB[2 + half], hnT1B[t]], writes=[bankB[bi]])
                    k.op("act", lambda: nc.scalar.activation(out=vn[t][:, hs_], in_=bank(bi), func=AF.Gelu), reads=[bankB[bi]], writes=[vnB[t][half]])
                if t % 4 == 0:
                    vst[t // 4] = stat_set(dedicated=t // 4)
                sumsq(vst[t // 4], t % 4, vn[t], vnB[t])
                if t % 4 == 3:
                    vY[t // 4] = newton(vst[t // 4], 4, 1.0 / D)

            for t in range(8):
                V_stage(t)
            if blk == 1:
                w13B_ = FFB["w13"]
                k.acquire("pool", writes=[w_uvB[3]])
                f_w13 = view(176, [128, 2, 8, 256], BF16)
                k.dma("pool", f_w13[:, 0, :, :], w1_d[1][:, 0:256].rearrange("(k p) c -> p k c", p=128), writes=[w13B_[0][0]])
                k.dma("pool", f_w13[:, 1, :, :], w3_d[1][:, 0:256].rearrange("(k p) c -> p k c", p=128), writes=[w13B_[0][1]])
            for c in range(8):
                if c == 4 and blk == 1:
                    cr0 = carry_of([w_uvB[0]])
                    F1["gvF"] = view(68, [128, 1024], F32)
                    F1["hn2"] = [view(64, [128, 1024], BF16), view(66, [128, 1024], BF16)]
                    F1["hn2B"] = [Buf("f1hn2a"), Buf("f1hn2b")]
                    for nb in F1["hn2B"]:
                        nb.r = dict(cr0)
                    k.acquire("sp", writes=[w_uvB[0]])
                    k.dma("sp", F1["gvF"], gvecs_d[3:4, :].partition_broadcast(128), writes=[FFB["gvF"]])
                if c == 7:
                    S_a(0)
                U_stage(c)
            S_b(0)
            if blk == 1:
                cr1 = carry_of([w_uvB[1], w_uvB[2]])
                F1["hnT"] = view(72, [128, 8, 1024], BF16)
                F1["hnTB_"] = [Buf("f1hnT") for _ in range(8)]
                for nb in F1["hnTB_"]:
                    nb.r = dict(cr1)
            for t in range(1, 8):
                S_a(t)
                S_b(t)
                if t >= 2:
                    mix_O(blk, t - 2)
                if blk == 0:
                    nt_tr(hn2m[(t - 1) % 2], hn2mB[(t - 1) % 2], t - 1, hnT1, hnT1B)
                else:
                    nt_tr(F1["hn2"][(t - 1) % 2], F1["hn2B"][(t - 1) % 2], t - 1, F1["hnT"], F1["hnTB_"])
            mix_O(blk, 6)
            mix_O(blk, 7)
            if blk == 0:
                nt_tr(hn2m[1], hn2mB[1], 7, hnT1, hnT1B)
            else:
                nt_tr(F1["hn2"][1], F1["hn2B"][1], 7, F1["hnT"], F1["hnTB_"])
        k.barrier()
        if stage <= 6:
            dump_h()
            return nc

        def make_final(old_bufs):
            carry = {}
            for ob in old_bufs:
                for ev in ([ob.w] if ob.w is not None else []) + list(ob.r.values()):
                    if ev[2] not in carry or carry[ev[2]][1] < ev[1]:
                        carry[ev[2]] = ev
            gfin = view(64, [128, 1024], F32)
            ot = [view(68 + 4 * i, [128, 1024], F32) for i in range(4)]
            gfinB = Buf("gfin")
            otB = [Buf("ot%d" % i) for i in range(4)]
            stB = [Buf("st%d" % i) for i in range(4)]
            for nb in [gfinB] + otB:
                nb.r = dict(carry)
            k.dma("sp", gfin, gvecs_d[4:5, :].partition_broadcast(128), writes=[gfinB])
            ctr = [0]

            def hook(t):
                tl = []
                if t < 8:
                    tl.append(t)
                if t >= 1:
                    tl.append(8 + t - 1)
                items = []
                for tile in tl:
                    i = ctr[0] % 4
                    ctr[0] += 1
                    items.append((tile, i))
                rms_batch([(hres[:, tile, :], hB[tile], ot[i], [otB[i]]) for tile, i in items], gfin, gfinB)
                for tile, i in items:
                    k.dma("sp", out_d[tile * 128:(tile + 1) * 128, :], ot[i], reads=[otB[i]], writes=[stB[i]])
            return hook

        ffn(1, 3, final=make_final, stY0=MX.get("stY_ffn1"), pre=dict(hn2B=F1["hn2B"], hnTB_=F1["hnTB_"]))
        k.barrier()
    return nc


def _t5_bucket(rel):
    half, max_exact = 16, 8
    ret = (rel > 0).astype(np.int32) * half
    n = np.abs(rel)
    nf = np.maximum(n, 1).astype(np.float32)
    large = max_exact + (np.log(nf / np.float32(max_exact)) / np.float32(math.log(128 / max_exact))
                         * np.float32(half - max_exact)).astype(np.int32)
    large = np.minimum(large, half - 1)
    return ret + np.where(n < max_exact, n, large)


def _const_tables(hf):
    s_true = np.arange(2048, dtype=np.int64) + hf * 2048
    jj = np.arange(512, dtype=np.int64)
    kk = hf * 2048 + np.stack([4 * jj, 4 * jj + 2, 2 * jj + 1, 1024 + 2 * jj + 1])
    prod = (s_true[None, :, None] * kk[:, None, :]) % S
    ang = prod.astype(np.float64) * (2.0 * np.pi / S)
    dftm = np.stack([np.cos(ang), np.sin(ang)], axis=2) / 64.0
    dftm = np.ascontiguousarray(dftm).astype(ml_dtypes.bfloat16)
    c = np.arange(128, dtype=np.int64)
    a2 = ((c[:, None] * c[None, :]) % 128).astype(np.float64) * (2.0 * np.pi / 128)
    cdft = np.stack([np.cos(a2), -np.sin(a2)], axis=1) / math.sqrt(128.0)
    return dftm, np.ascontiguousarray(cdft).astype(ml_dtypes.bfloat16)


_CONST_CACHE = {}


def _prepare(inputs):
    f32 = np.float32
    g = lambda n: np.asarray(inputs[n], dtype=f32)
    x = g("x")
    rel_table = g("rel_bias_table")
    shared = {
        "w_in": np.ascontiguousarray(g("even_w_in")[0]),
        "w_out0": np.ascontiguousarray(g("even_w_out")[0]),
        "w_uv": np.ascontiguousarray(g("odd_w_uv")[0]),
        "w_out1": np.ascontiguousarray(g("odd_w_out")[0]),
        "gvecs": np.ascontiguousarray(np.stack([g("norm_mix_g")[0], g("norm_ffn_g")[0], g("norm_mix_g")[1],
                                                g("norm_ffn_g")[1], g("final_norm_g"), g("odd_v_norm_g")[0]])),
        "lam": np.ascontiguousarray(g("diff_lambda")[0].reshape(1, 256)),
        "subg": np.ascontiguousarray(g("diff_subln_g")[0].reshape(1, 128)),
        "wsT": np.ascontiguousarray(g("odd_w_s")[0].transpose(2, 0, 1)),
        "bs": np.ascontiguousarray(g("odd_b_s")[0].reshape(1, 1024)),
    }
    for l in range(2):
        shared["w1_%d" % l] = np.ascontiguousarray(g("ffn_w1")[l])
        shared["w3_%d" % l] = np.ascontiguousarray(g("ffn_w3")[l])
        shared["w2_%d" % l] = np.ascontiguousarray(g("ffn_w2")[l])
    p = np.arange(128)[:, None]
    j = np.arange(1152)[None, :]
    qf = np.arange(512)[None, :]
    per_half = []
    for hf in range(2):
        if hf not in _CONST_CACHE:
            _CONST_CACHE[hf] = _const_tables(hf)
        dftm, cdft = _CONST_CACHE[hf]
        wown = rel_table[_t5_bucket(p - j + 512)]
        wown = np.ascontiguousarray(wown.transpose(0, 2, 1))
        sh = 0 if hf == 0 else -S
        wo0 = rel_table[_t5_bucket(512 + p - qf + sh)]
        wo1 = rel_table[_t5_bucket(3968 + p - qf + sh)]
        woth = np.ascontiguousarray(np.stack([wo0, wo1], axis=0).transpose(1, 3, 0, 2))
        bc = np.zeros((128, 12), f32)
        for h in range(4):
            bc[:, 3 * h + 0] = rel_table[15, h]
            bc[:, 3 * h + 1] = rel_table[31, h]
            bc[:, 3 * h + 2] = rel_table[31 if hf == 0 else 15, h]
        per_half.append(dict(dftm=dftm, cdft=cdft, wown=wown, woth=woth, bconst=bc))
    in_maps = []
    for c in range(8):
        b, hf = c // 2, c % 2
        xc = np.concatenate([x[b, hf * 2048:(hf + 1) * 2048], x[b, (1 - hf) * 2048:(2 - hf) * 2048]], axis=0)
        m = dict(shared)
        m.update(per_half[hf])
        m["x"] = np.ascontiguousarray(xc)
        in_maps.append(m)
    return in_maps


_NC_CACHE = {}


def kernel(**inputs):
    in_maps = _prepare(inputs)
    if "nc" not in _NC_CACHE:
        _NC_CACHE["nc"] = build()
    nc = _NC_CACHE["nc"]
    res = run_bass_kernel_spmd(nc, in_maps, core_ids=list(range(8)))
    out = np.empty((NB, S, D), np.float32)
    for c in range(8):
        b, hf = c // 2, c % 2
        out[b, hf * 2048:(hf + 1) * 2048] = np.asarray(res.results[c]["out"], dtype=np.float32)
    return out
```

```python
import math
import numpy as np
import ml_dtypes
from contextlib import ExitStack
import concourse.bass as bass
import concourse.mybir as mybir
from concourse.bass_utils import run_bass_kernel_spmd

F32 = mybir.dt.float32
BF16 = mybir.dt.bfloat16
I32 = mybir.dt.int32
AF = mybir.ActivationFunctionType
ALU = mybir.AluOpType

D = 1024
S = 4096
NB = 4
DFF = 2816
NJ = DFF // 128
EPS = 1e-6
LAM_INIT0 = 0.8 - 0.6 * math.exp(-0.3 * 0)
KIB = 1024


class Buf:
    __slots__ = ("name", "w", "r", "sem", "cnt", "key")

    def __init__(self, name):
        self.name = name
        self.w = None
        self.r = {}
        self.sem = None
        self.cnt = 0
        self.key = None


class KB:
    def __init__(self, nc, es):
        self.nc = nc
        self.es = es
        self.eng = dict(pe=nc.tensor, act=nc.scalar, dve=nc.vector, pool=nc.gpsimd, sp=nc.sync)
        self.esem = {k: es.enter_context(nc.semaphore("e_" + k)) for k in self.eng}
        self.ecnt = {k: 0 for k in self.eng}
        self.seen = {k: {} for k in self.eng}
        self.dbufs = []
        self.load = dict(act=0.0, dve=0.0)

    def _wait(self, e, evs):
        best = {}
        for ev in evs:
            if ev is None:
                continue
            sem, val, key = ev
            if self.seen[e].get(key, 0) >= val:
                continue
            if key not in best or best[key][1] < val:
                best[key] = ev
        for key, (sem, val, _) in best.items():
            self.eng[e].wait_ge(sem, val)
            self.seen[e][key] = val

    @staticmethod
    def _deps(reads, writes):
        evs = []
        for b in reads:
            if b.w is not None:
                evs.append(b.w)
        for b in writes:
            if b.w is not None:
                evs.append(b.w)
            evs.extend(b.r.values())
        return evs

    @staticmethod
    def _commit(ev, reads, writes):
        for b in reads:
            b.r[ev[2]] = ev
        for b in writes:
            b.w = ev
            b.r = {}

    def acquire(self, e, reads=(), writes=()):
        self._wait(e, self._deps(reads, writes))

    def _signal(self, e, inst):
        self.ecnt[e] += 1
        inst.then_inc(self.esem[e], 1)
        return (self.esem[e], self.ecnt[e], e)

    def op(self, e, fn, reads=(), writes=()):
        self._wait(e, self._deps(reads, writes))
        ev = self._signal(e, fn())
        self._commit(ev, reads, writes)
        return ev

    def group(self, e, fns, reads=(), writes=(), acquire=True):
        if acquire:
            self._wait(e, self._deps(reads, writes))
        inst = None
        for fn in fns:
            inst = fn()
        ev = self._signal(e, inst)
        self._commit(ev, reads, writes)
        return ev

    def dma(self, q, out, in_, reads=(), writes=(), sembuf=None, waw=True):
        deps = self._deps(reads, writes)
        if not waw:
            ws = {id(b.w) for b in writes if b.w is not None}
            deps = [e for e in deps if id(e) not in ws]
        self._wait(q, deps)
        sb = sembuf if sembuf is not None else (writes[0] if writes else reads[0])
        if sb.sem is None:
            sb.key = "d%d" % len(self.dbufs)
            sb.sem = self.es.enter_context(self.nc.semaphore("s_" + sb.key))
            self.dbufs.append(sb)
        sb.cnt += 16
        self.eng[q].dma_start(out=out, in_=in_).then_inc(sb.sem, 16)
        ev = (sb.sem, sb.cnt, sb.key)
        self._commit(ev, reads, writes)
        return ev

    def barrier(self):
        evs = [(self.esem[e], self.ecnt[e], e) for e in self.eng if self.ecnt[e] > 0]
        evs += [(b.sem, b.cnt, b.key) for b in self.dbufs if b.cnt > 0]
        for e in self.eng:
            self._wait(e, evs)

    def pick(self, cost):
        e = "act" if self.load["act"] <= self.load["dve"] else "dve"
        self.load[e] += cost
        return e


def build(stage=99, debug=False):
    nc = bass.Bass("TRN2", target_bir_lowering=False)

    def din(name, shape, dt=F32):
        return nc.dram_tensor(name, list(shape), dt, kind="ExternalInput").ap()

    x = din("x", [S, D])
    w_in_d = din("w_in", [D, 2048])
    w_out0_d = din("w_out0", [D, D])
    w_uv_d = din("w_uv", [D, 2048])
    w_out1_d = din("w_out1", [D, D])
    w1_d = [din("w1_%d" % l, [D, DFF]) for l in range(2)]
    w3_d = [din("w3_%d" % l, [D, DFF]) for l in range(2)]
    w2_d = [din("w2_%d" % l, [DFF, D]) for l in range(2)]
    gvecs_d = din("gvecs", [6, D])
    lam_d = din("lam", [1, 256])
    subg_d = din("subg", [1, 128])
    dftm_d = din("dftm", [4, 2048, 2, 512], BF16)
    cdft_d = din("cdft", [128, 2, 128], BF16)
    wown_d = din("wown", [128, 4, 1152])
    woth_d = din("woth", [128, 4, 2, 512])
    bconst_d = din("bconst", [128, 12])
    wsT_d = din("wsT", [128, 8, 128])
    bs_d = din("bs", [1, 1024])
    out_d = nc.dram_tensor("out", [2048, D], F32, kind="ExternalOutput").ap()
    dbg = {}
    if debug:
        for nm, shp, dt in [("d_kt", [128, 4, S], BF16), ("d_qt", [128, 4, 2048], BF16),
                            ("d_v", [128, 32, 4, 132], BF16), ("d_z", [128, 32, 512], BF16),
                            ("d_ft", [128, 4, 2048], BF16), ("d_a", [128, 16, 512], BF16)]:
            dbg[nm] = nc.dram_tensor(nm, shp, dt, kind="ExternalOutput").ap()

    es = ExitStack()
    with es:
        arena = es.enter_context(nc.sbuf_tensor("arena", [128, 207 * 512], BF16))
        psum = es.enter_context(nc.psum_tensor("psum", [128, 4096], F32))
        k = KB(nc, es)

        def view(off_kib, shape, dt):
            esz = 2 if dt == BF16 else 4
            n = 1
            for s_ in shape[1:]:
                n *= s_
            o = int(off_kib * 512)
            a = arena[:, o:o + n * esz // 2]
            if dt == F32:
                a = a.bitcast(F32)
            if len(shape) == 3:
                a = a.rearrange("p (a b) -> p a b", a=shape[1])
            elif len(shape) == 4:
                a = a.rearrange("p (a b c) -> p a b c", a=shape[1], b=shape[2])
            return a

        def bank(i, n=512):
            return psum[:, i * 512:i * 512 + n]

        def bank_bf(i):
            return psum[:, i * 512:(i + 1) * 512].bitcast(BF16)

        bankB = [Buf("bank%d" % i) for i in range(8)]
        bank_rr = [0]

        def next_bank():
            i = bank_rr[0] % 8
            bank_rr[0] += 1
            return i

        def evac(out, in_, reads, writes, scale=None, n=512, eng=None):
            e = eng or k.pick(0.25 + n / 1000.0)
            if e == "act":
                if scale is None:
                    return k.op("act", lambda: nc.scalar.copy(out=out, in_=in_), reads, writes)
                return k.op("act", lambda: nc.scalar.activation(out=out, in_=in_, func=AF.Copy, scale=float(scale)), reads, writes)
            if scale is None:
                return k.op("dve", lambda: nc.vector.tensor_copy(out=out, in_=in_), reads, writes)
            return k.op("dve", lambda: nc.vector.tensor_scalar(out=out, in0=in_, scalar1=float(scale), scalar2=None, op0=ALU.mult), reads, writes)

        ident = view(200, [128, 128], BF16)
        identf = view(200.25, [128, 128], F32)
        junk = view(201, [128, 1024], BF16)
        junkD = view(206, [128, 128], BF16)
        junkAB = Buf("junkA")
        junkDB = Buf("junkD")
        small = view(203, [128, 256], F32)
        gsub = view(204, [128, 128], F32)
        lam_sb = view(204.5, [128, 256], F32)
        bconst = view(205.5, [128, 12], F32)
        identB = Buf("ident")
        smallB = Buf("small")
        gsubB = Buf("gsub")
        lamB = Buf("lam")
        bconstB = Buf("bconst")
        ss_ap = small[:, 0:8]
        rt_ap = small[:, 8:16]
        rstd_ap = small[:, 16:24]
        nlam = small[:, 24:25]
        lsum = small[:, 26:28]
        lexp = small[:, 28:30]
        ltmp = small[:, 30:31]

        k.op("pool", lambda: nc.gpsimd.memset(identf, 0.0), writes=[identB])
        k.op("pool", lambda: nc.gpsimd.affine_select(out=identf, in_=identf, compare_op=ALU.not_equal, fill=1.0,
                                                      base=0, pattern=[[-1, 128]], channel_multiplier=1),
             reads=[identB], writes=[identB])
        k.op("dve", lambda: nc.vector.tensor_copy(out=ident, in_=identf), reads=[identB], writes=[identB])
        k.dma("sp", lam_sb, lam_d.partition_broadcast(128), writes=[lamB])
        k.dma("sp", gsub, subg_d.partition_broadcast(128), writes=[gsubB])
        k.dma("sp", bconst, bconst_d, writes=[bconstB])
        k.op("dve", lambda: nc.vector.scalar_tensor_tensor(out=junkD[:, 0:64], in0=lam_sb[:, 0:64], scalar=1.0, in1=lam_sb[:, 64:128],
                                                            op0=ALU.mult, op1=ALU.mult, accum_out=lsum[:, 0:1]), reads=[lamB], writes=[smallB, junkDB])
        k.op("dve", lambda: nc.vector.scalar_tensor_tensor(out=junkD[:, 64:128], in0=lam_sb[:, 128:192], scalar=1.0, in1=lam_sb[:, 192:256],
                                                            op0=ALU.mult, op1=ALU.mult, accum_out=lsum[:, 1:2]), reads=[lamB, smallB], writes=[smallB, junkDB])
        k.op("act", lambda: nc.scalar.activation(out=lexp, in_=lsum, func=AF.Exp), reads=[smallB], writes=[smallB])
        k.op("dve", lambda: nc.vector.tensor_tensor(out=ltmp, in0=lexp[:, 1:2], in1=lexp[:, 0:1], op=ALU.subtract), reads=[smallB], writes=[smallB])
        k.op("dve", lambda: nc.vector.tensor_scalar(out=nlam, in0=ltmp, scalar1=-LAM_INIT0, scalar2=None, op0=ALU.add), reads=[smallB], writes=[smallB])
        k.op("dve", lambda: nc.vector.tensor_scalar(out=gsub, in0=gsub, scalar1=1.0 - LAM_INIT0, scalar2=None, op0=ALU.mult), reads=[gsubB], writes=[gsubB])

        NSET = 2
        statB = [Buf("stat%d" % i) for i in range(NSET + 4)]
        stat_rr = [0]

        def stat_set(dedicated=None):
            if dedicated is None:
                i = stat_rr[0] % NSET
                stat_rr[0] += 1
            else:
                i = NSET + dedicated
            b0 = 64 + 32 * i
            return (small[:, b0:b0 + 8], small[:, b0 + 8:b0 + 16], small[:, b0 + 16:b0 + 24], small[:, b0 + 24:b0 + 32], statB[i])

        def sumsq(st, col, src, srcB):
            k.op("act", lambda: nc.scalar.activation(out=junk, in_=src, func=AF.Square, accum_out=st[0][:, col:col + 1]), reads=srcB, writes=[st[4], junkAB])

        def newton(st, n, inv_d):
            SS, X, Y, T, sB = st
            SS, X, Y, T = SS[:, 0:n], X[:, 0:n], Y[:, 0:n], T[:, 0:n]
            Xi, Yi = X.bitcast(I32), Y.bitcast(I32)
            d = lambda fn: k.op("dve", fn, reads=[sB], writes=[sB])
            d(lambda: nc.vector.tensor_scalar(out=X, in0=SS, scalar1=float(inv_d), scalar2=EPS, op0=ALU.mult, op1=ALU.add))
            d(lambda: nc.vector.tensor_scalar(out=Yi, in0=Xi, scalar1=1, scalar2=None, op0=ALU.arith_shift_right))
            d(lambda: nc.vector.tensor_scalar(out=Yi, in0=Yi, scalar1=-1.0, scalar2=float(0x5f3759df), op0=ALU.mult, op1=ALU.add))
            for _ in range(3):
                d(lambda: nc.vector.tensor_tensor(out=T, in0=Y, in1=Y, op=ALU.mult))
                d(lambda: nc.vector.tensor_tensor(out=T, in0=T, in1=X, op=ALU.mult))
                d(lambda: nc.vector.tensor_scalar(out=T, in0=T, scalar1=-0.5, scalar2=1.5, op0=ALU.mult, op1=ALU.add))
                d(lambda: nc.vector.tensor_tensor(out=Y, in0=Y, in1=T, op=ALU.mult))
            return Y

        def rms_batch(items, gv, gvB):
            st = stat_set()
            for i, (src_, srcB_, _, _) in enumerate(items):
                sumsq(st, i, src_, srcB_)
            Y = newton(st, len(items), 1.0 / D)
            for i, (src_, srcB_, dst_, dstB_) in enumerate(items):
                k.op("dve", lambda: nc.vector.scalar_tensor_tensor(out=dst_, in0=src_, scalar=Y[:, i:i + 1], in1=gv, op0=ALU.mult, op1=ALU.mult),
                     reads=list(srcB_) + [st[4], gvB], writes=dstB_)

        nt_ctr = [0]

        def nt_stats(tiles):
            st = stat_set(dedicated=2 + nt_ctr[0] % 2)
            nt_ctr[0] += 1
            for i, tile in enumerate(tiles):
                sumsq(st, i, hres[:, tile, :], hB[tile])
            Y = newton(st, len(tiles), 1.0 / D)
            return st, Y

        def nt_apply(stY, i, tile, gv, gvB, slot, slotB):
            st, Y = stY
            k.op("dve", lambda: nc.vector.scalar_tensor_tensor(out=slot, in0=hres[:, tile, :], scalar=Y[:, i:i + 1], in1=gv, op0=ALU.mult, op1=ALU.mult),
                 reads=list(hB[tile]) + [st[4], gvB], writes=[slotB])

        def nt_tr(slot, slotB, t, hnT_, hnTB_):
            bi = next_bank()
            bb = bank_bf(bi)
            k.group("pe", [(lambda kk=kk: nc.tensor.transpose(out=bb[:, kk * 128:(kk + 1) * 128], in_=slot[:, kk * 128:(kk + 1) * 128], identity=ident))
                           for kk in range(8)], reads=[slotB, identB], writes=[bankB[bi]])
            evac(hnT_[:, :, t * 128:(t + 1) * 128], bb.rearrange("p (a b) -> p a b", a=8), [bankB[bi]], [hnTB_[t]], n=1024)

        def nt_block_now(tiles, gv, gvB, slots, slotsB, hnT_, hnTB_, stY=None):
            if stY is None:
                stY = nt_stats(tiles)
            ns = len(slots)
            warm_pe(16)
            for i, tile in enumerate(tiles):
                nt_apply(stY, i, tile, gv, gvB, slots[i % ns], slotsB[i % ns])
                if i >= 1:
                    nt_tr(slots[(i - 1) % ns], slotsB[(i - 1) % ns], i - 1, hnT_, hnTB_)
                    warm_pe(4)
            nt_tr(slots[(len(tiles) - 1) % ns], slotsB[(len(tiles) - 1) % ns], len(tiles) - 1, hnT_, hnTB_)

        eps_t = view(205.75, [128, 2], F32)
        epsB = Buf("eps")
        k.op("dve", lambda: nc.vector.memset(eps_t, EPS), writes=[epsB])
        EPS_AP = eps_t[:, 0:1]

        warm = view(206.25, [128, 256], BF16)
        warmB = Buf("warm")
        k.op("dve", lambda: nc.vector.memset(warm, 0.0), writes=[warmB])

        def warm_pe(n):
            bi = next_bank()
            k.group("pe", [(lambda: nc.tensor.matmul(bank(bi, 256), lhsT=ident, rhs=warm, start=True, stop=True)) for _ in range(n)],
                    reads=[identB, warmB], writes=[bankB[bi]])

        w_in = view(0, [128, 8, 2048], BF16)
        xt = [view(o_, [128, 2, 1024], F32) for o_ in (32, 40, 184)]
        hn = [view(o_, [128, 2, 1024], BF16) for o_ in (48, 52, 192)]
        hnT = [view(56 + 4 * i, [128, 8, 256], BF16) for i in range(2)]
        KT = view(64, [128, 4, S], BF16)
        Vaug = view(96, [128, 32, 4, 132], BF16)
        QT = view(130, [128, 4, 2048], BF16)
        Z = view(146, [128, 32, 512], BF16)
        gvec = view(178, [128, 1024], F32)
        fT = view(184, [128, 4, 2048], BF16)

        w_inB = [Buf("w_in%d" % i) for i in range(4)]
        xtB = [Buf("xt%d" % i) for i in range(3)]
        hnB = [[Buf("hn") for _ in range(2)] for _ in range(3)]
        hnTB = [[Buf("hnT") for _ in range(2)] for _ in range(2)]
        KTB = [[Buf("KT") for _ in range(16)] for _ in range(4)]
        QTB = [[Buf("QT") for _ in range(8)] for _ in range(4)]
        VB = [Buf("V") for _ in range(32)]
        ZB = [Buf("Z") for _ in range(32)]
        gvecB = Buf("gvec")
        onesB = Buf("ones")

        k.dma("sp", gvec, gvecs_d[0:1, :].partition_broadcast(128), writes=[gvecB])
        for cb in (2, 1, 0, 3):
            k.dma("pool", w_in[:, :, cb * 512:(cb + 1) * 512],
                  w_in_d[:, cb * 512:(cb + 1) * 512].rearrange("(k p) c -> p k c", p=128), writes=[w_inB[cb]])
            if cb == 2:
                k.acquire("pool", reads=[w_inB[2]])
        k.op("dve", lambda: nc.vector.memset(Vaug[:, :, :, 128:129], 1.0), writes=[onesB])

        def load_x(g):
            k.dma("sp", xt[g % 3], x[g * 256:(g + 1) * 256, :].rearrange("(t p) d -> p t d", p=128), writes=[xtB[g % 3]])

        def normA(g):
            s3 = g % 3
            rms_batch([(xt[s3][:, t, :], [xtB[s3]], hn[s3][:, t, :], [hnB[s3][t]]) for t in range(2)], gvec, gvecB)

        def transA(g):
            sl = g % 2
            s3 = g % 3
            for t in range(2):
                bi = next_bank()
                bb = bank_bf(bi)
                k.group("pe", [(lambda kk=kk: nc.tensor.transpose(out=bb[:, kk * 128:(kk + 1) * 128], in_=hn[s3][:, t, kk * 128:(kk + 1) * 128], identity=ident))
                               for kk in range(8)], reads=[hnB[s3][t], identB], writes=[bankB[bi]])
                evac(hnT[sl][:, :, t * 128:(t + 1) * 128], bb.rearrange("p (a b) -> p a b", a=8), [bankB[bi]], [hnTB[sl][t]], n=1024)

        load_x(0)
        k.acquire("sp", reads=[xtB[0]])
        load_x(1)
        load_x(2)
        normA(0)
        normA(1)
        transA(0)
        for g in range(16):
            own = g < 8
            sl = g % 2
            if g + 3 < 16:
                load_x(g + 3)
            if g + 2 < 16:
                normA(g + 2)
            hT = hnT[sl]
            hTB = hnTB[sl]
            for which in ("k", "q"):
                if which == "q" and not own:
                    continue
                for h in range(4):
                    c0 = (1024 if which == "k" else 512) + h * 128
                    bi = next_bank()
                    bk = bank(bi, 256)
                    k.group("pe", [(lambda kk=kk: nc.tensor.matmul(bk, lhsT=w_in[:, kk, c0:c0 + 128], rhs=hT[:, kk, :], start=(kk == 0), stop=(kk == 7)))
                                   for kk in range(8)], reads=[w_inB[2 if which == "k" else 1]] + hTB, writes=[bankB[bi]])
                    if which == "k":
                        evac(KT[:, h, g * 256:(g + 1) * 256], bk, [bankB[bi]], [KTB[h][g]], n=256)
                    else:
                        evac(QT[:, h, g * 256:(g + 1) * 256], bk, [bankB[bi]], [QTB[h][g]], scale=0.125, n=256)
            if g + 1 < 16:
                transA(g + 1)
            for t in range(2):
                tile = g * 2 + t
                for which in ("z", "v"):
                    c0 = 0 if which == "z" else 1536
                    bi = next_bank()
                    bk = bank(bi)
                    k.group("pe", [(lambda kk=kk: nc.tensor.matmul(bk, lhsT=hT[:, kk, t * 128:(t + 1) * 128], rhs=w_in[:, kk, c0:c0 + 512], start=(kk == 0), stop=(kk == 7)))
                                   for kk in range(8)], reads=[w_inB[0 if which == "z" else 3], hTB[t]], writes=[bankB[bi]])
                    if which == "z":
                        evac(Z[:, tile, :], bk, [bankB[bi]], [ZB[tile]])
                    else:
                        evac(Vaug[:, tile, :, 0:128], bk.rearrange("p (h e) -> p h e", h=4), [bankB[bi]], [VB[tile]])
        k.barrier()
        if debug:
            k.dma("sp", dbg["d_kt"], KT, reads=[KTB[0][0]], sembuf=Buf("dbg"))
            k.dma("sp", dbg["d_qt"], QT, reads=[KTB[0][0]], sembuf=Buf("dbg"))
            k.dma("sp", dbg["d_v"], Vaug, reads=[KTB[0][0]], sembuf=Buf("dbg"))
            k.dma("sp", dbg["d_z"], Z, reads=[KTB[0][0]], sembuf=Buf("dbg"))
            k.barrier()
        if stage <= 1:
            return nc

        ring = [view(2 * i, [128, 2, 512], BF16) for i in range(16)]
        UV = view(56, [128, 8, 512], BF16)
        cdft = view(182, [128, 2, 128], BF16)
        ringB = [Buf("ring%d" % i) for i in range(16)]
        UVB = [Buf("UV%d" % i) for i in range(8)]
        cdftB = Buf("cdft")
        fTB = [[Buf("fT") for _ in range(4)] for _ in range(4)]
        k.dma("sp", cdft, cdft_d, writes=[cdftB])

        Zm = view(32, [128, 16, 512], BF16)
        Zpm = view(48, [128, 8, 512], BF16)
        ZmB = [Buf("Zm%d" % i) for i in range(16)]
        ZpmB = [Buf("Zpm%d" % i) for i in range(8)]
        for a in range(16):
            k.op("dve", lambda: nc.vector.tensor_tensor(out=Zm[:, a, :], in0=Z[:, a, :], in1=Z[:, a + 16, :], op=ALU.subtract),
                 reads=[ZB[a], ZB[a + 16]], writes=[ZmB[a]])
            k.op("dve", lambda: nc.vector.tensor_tensor(out=Z[:, a, :], in0=Z[:, a, :], in1=Z[:, a + 16, :], op=ALU.add),
                 reads=[ZB[a + 16]], writes=[ZB[a]])
        for a in range(8):
            k.op("dve", lambda: nc.vector.tensor_tensor(out=Zpm[:, a, :], in0=Z[:, a, :], in1=Z[:, a + 8, :], op=ALU.subtract),
                 reads=[ZB[a], ZB[a + 8]], writes=[ZpmB[a]])
            k.op("dve", lambda: nc.vector.tensor_tensor(out=Z[:, a, :], in0=Z[:, a, :], in1=Z[:, a + 8, :], op=ALU.add),
                 reads=[ZB[a + 8]], writes=[ZB[a]])

        chunk_src = {0: (Z, ZB, 8, slice(0, 2048, 4)), 1: (Zpm, ZpmB, 8, slice(2, 2048, 4)),
                     2: (Zm, ZmB, 16, slice(1, 1024, 2)), 3: (Zm, ZmB, 16, slice(1025, 2048, 2))}
        corder = (2, 3, 0, 1)

        def load_piece(c_, a_):
            k.dma("sp", ring[a_], dftm_d[c_, a_ * 128:(a_ + 1) * 128, :, :], writes=[ringB[a_]])

        for a_ in range(chunk_src[corder[0]][2]):
            load_piece(corder[0], a_)
        hidx = 0
        pending = []
        for ci, c in enumerate(corder):
            zsrc, zsrcB, na, osl = chunk_src[c]
            nxt = corder[ci + 1] if ci + 1 < 4 else None
            for half in range(2):
                b0 = 4 * (hidx % 2)
                hidx += 1
                gs = (2 * half, 2 * half + 1)
                hb = [bankB[b0 + j] for j in range(4)]
                k.acquire("pe", writes=hb)
                ev = None
                for a in range(na):
                    rg = ring[a]
                    fns = []
                    for j, g in enumerate(gs):
                        zl = zsrc[:, a, g * 128:(g + 1) * 128]
                        fns.append(lambda j=j, zl=zl, rg=rg: nc.tensor.matmul(bank(b0 + j), lhsT=zl, rhs=rg[:, 0, :], start=(a == 0), stop=(a == na - 1)))
                        fns.append(lambda j=j, zl=zl, rg=rg: nc.tensor.matmul(bank(b0 + 2 + j), lhsT=zl, rhs=rg[:, 1, :], start=(a == 0), stop=(a == na - 1)))
                    ev = k.group("pe", fns, reads=[ringB[a], zsrcB[a]], writes=[])
                    if half == 1 and nxt is not None and a < chunk_src[nxt][2]:
                        load_piece(nxt, a)
                    if a == 3 and pending:
                        pending.pop(0)()
                for b_ in hb:
                    b_.w = ev
                    b_.r = {}
                for j in range(4):
                    evac(UV[:, b0 + j, :], bank(b0 + j), [bankB[b0 + j]], [UVB[b0 + j]])

                def chan(b0=b0, gs=gs, osl=osl, c=c):
                    for j, g in enumerate(gs):
                        k.group("pe", [lambda j=j: nc.tensor.matmul(bank(b0 + j), lhsT=cdft[:, 0, :], rhs=UV[:, b0 + j, :], start=True, stop=False),
                                       lambda j=j: nc.tensor.matmul(bank(b0 + j), lhsT=cdft[:, 1, :], rhs=UV[:, b0 + 2 + j, :], start=False, stop=True)],
                                reads=[UVB[b0 + j], UVB[b0 + 2 + j], cdftB], writes=[bankB[b0 + j]])
                        evac(fT[:, g, osl], bank(b0 + j), [bankB[b0 + j]], [fTB[g][c]])
                pending.append(chan)
        while pending:
            pending.pop(0)()
        k.barrier()
        if debug:
            k.dma("sp", dbg["d_ft"], fT, reads=[KTB[0][0]], sembuf=Buf("dbg"))
            k.barrier()
        if stage <= 2:
            return nc

        wown = view(0, [128, 4, 1152], F32)
        woth = view(18, [128, 4, 2, 512], F32)
        Pb = [view(34 + 2 * i, [128, 1024], BF16) for i in range(3)]
        Tb = [view(40 + 4 * i, [128, 2, 512], F32) for i in range(2)]
        osb = [view(48 + 2 * i, [128, 4, 128], F32) for i in range(2)]
        o1t = [view(52 + 0.5 * i, [128, 128], F32) for i in range(2)]
        Osb = [view(53 + 4.25 * i, [128, 4, 264], F32) for i in range(2)]
        OsbB = [Buf("Osb%d" % i) for i in range(2)]
        a_tm = view(146, [128, 16, 512], BF16)
        aT = view(162, [128, 4, 2048], BF16)
        w_out0 = view(64, [128, 8, 1024], BF16)
        wownBh = [Buf("wown%d" % i) for i in range(4)]
        wothBh = [Buf("woth%d" % i) for i in range(4)]
        PB = [Buf("P%d" % i) for i in range(3)]
        TB = [[Buf("T%d_%d" % (i, m)) for m in range(2)] for i in range(2)]
        osB = [[Buf("os") for _ in range(4)] for _ in range(2)]
        o1B = [Buf("o1t%d" % i) for i in range(2)]
        aB = [[Buf("a") for _ in range(4)] for _ in range(16)]
        aTB = [Buf("aT%d" % i) for i in range(16)]
        w_out0B = Buf("w_out0")
        LB = [Buf("L0"), Buf("L1")]
        OB = [Buf("O%d" % i) for i in range(4)]
        rsB = Buf("rs")
        hsB = Buf("hs")
        rs = small[:, 32:40].rearrange("p (b j) -> p b j", b=4)
        hss = small[:, 40:44]
        hln = small[:, 44:48]
        hrs = small[:, 48:52]
        for h_ in range(4):
            k.dma("sp", wown[:, h_, :], wown_d[:, h_, :], writes=[wownBh[h_]])
            k.dma("sp", woth[:, h_, :, :], woth_d[:, h_, :, :], writes=[wothBh[h_]])
        Obank = psum[:, 2048:4096].rearrange("p (b c) -> p b c", b=4)
        Lq = [psum[:, 0:1024], psum[:, 1024:2048]]
        iters = [(h_, qc_) for h_ in range(4) for qc_ in range(4)]
        pcount = [0]
        pb_of = {}

        def order_of(qc_):
            s_ = {0: 5, 1: 9, 2: 13, 3: 17}[qc_]
            return [(s_ + p_) % 32 for p_ in range(32)]
        orders = [order_of(qc_) for (_, qc_) in iters]
        Lidx = {}

        def qk(it, pos):
            h, qc = iters[it]
            kt = orders[it][pos]
            i = (it * 32 + pos) % 2
            Lidx[(it, pos)] = i
            ks = slice(kt * 128, (kt + 1) * 128)
            qs = slice(qc * 512, (qc + 1) * 512)
            k.group("pe", [lambda: nc.tensor.matmul(Lq[i][:, 0:512], lhsT=KT[0:64, h, ks], rhs=QT[0:64, h, qs], start=True, stop=True),
                           lambda: nc.tensor.matmul(Lq[i][:, 512:1024], lhsT=KT[64:128, h, ks], rhs=QT[64:128, h, qs], start=True, stop=True)],
                    reads=[KTB[h][kt // 2]] + [QTB[h][2 * qc], QTB[h][2 * qc + 1]], writes=[LB[i]])

        def softmax_tile(it, pos):
            h, qc = iters[it]
            kt = orders[it][pos]
            pb = pcount[0] % 3
            pcount[0] += 1
            pb_of[(it, pos)] = pb
            i = Lidx[(it, pos)]
            argB = None
            if kt < 16:
                dl = kt * 128 - qc * 512
                if -255 < dl < 639:
                    mode, arg, argB = "near", wown[:, h, 512 - dl:1024 - dl], wownBh[h]
                else:
                    col = 3 * h + (1 if dl >= 639 else 0)
                    mode, arg = "far", bconst[:, col:col + 1]
            elif (kt, qc) == (16, 3):
                mode, arg, argB = "near", woth[:, h, 0, :], wothBh[h]
            elif (kt, qc) == (31, 0):
                mode, arg, argB = "near", woth[:, h, 1, :], wothBh[h]
            else:
                mode, arg = "far", bconst[:, 3 * h + 2:3 * h + 3]
            if mode == "far":
                k.op("act", lambda: nc.scalar.activation(out=Pb[pb], in_=Lq[i], func=AF.Exp, bias=arg, scale=1.0),
                     reads=[LB[i], bconstB], writes=[PB[pb]])
            else:
                tb = pos % 2
                for m in range(2):
                    ms = slice(m * 512, (m + 1) * 512)
                    k.op("dve", lambda: nc.vector.tensor_tensor(out=Tb[tb][:, m, :], in0=Lq[i][:, ms], in1=arg, op=ALU.add),
                         reads=[LB[i], argB], writes=[TB[tb][m]])
                k.op("act", lambda: nc.scalar.activation(out=Pb[pb], in_=Tb[tb].rearrange("p m q -> p (m q)"), func=AF.Exp),
                     reads=TB[tb], writes=[PB[pb]])

        def av(it, pos):
            h, qc = iters[it]
            kt = orders[it][pos]
            pb = pb_of[(it, pos)]
            fns = []
            for m in range(2):
                for qt in range(4):
                    bnk = 2 * m + qt // 2
                    j = qt % 2
                    fns.append(lambda m=m, qt=qt, bnk=bnk, j=j: nc.tensor.matmul(
                        Obank[:, bnk, j * 129:(j + 1) * 129], lhsT=Pb[pb][:, m * 512 + qt * 128:m * 512 + (qt + 1) * 128],
                        rhs=Vaug[:, kt, h, 0:129], start=(pos == 0 and j == 0), stop=(pos == 31), skip_group_check=True))
            if pos == 0:
                k.acquire("pe", writes=OB[0:2])
                k.group("pe", fns[0:4], reads=[PB[pb], VB[kt], onesB], writes=[])
                k.acquire("pe", writes=OB[2:4])
                fns = fns[4:]
            ev_ = k.group("pe", fns, reads=[PB[pb], VB[kt], onesB], writes=[])
            if pos == 31:
                for b_ in OB:
                    b_.w = ev_
                    b_.r = {}

        epi_st = {}

        def epi0(it):
            e = it % 2
            for b_ in range(4):
                k.op("dve", lambda: nc.vector.tensor_copy(out=Osb[e][:, b_, 0:258], in_=Obank[:, b_, 0:258]), reads=[OB[b_]], writes=[OsbB[e]])

        def epi1(it):
            e = it % 2
            Ob = Osb[e]
            k.op("dve", lambda: nc.vector.reciprocal(out=rs, in_=Ob[:, :, 128:258:129]), reads=[OsbB[e]], writes=[rsB])
            k.op("dve", lambda: nc.vector.tensor_scalar(out=rs[:, 2:4, :], in0=rs[:, 2:4, :], scalar1=nlam, scalar2=None, op0=ALU.mult),
                 reads=[rsB, smallB], writes=[rsB])
            for qt in range(4):
                pr, j = qt // 2, qt % 2
                k.op("dve", lambda: nc.vector.tensor_scalar(out=o1t[qt % 2], in0=Ob[:, pr, j * 129:j * 129 + 128], scalar1=rs[:, pr, j:j + 1],
                                                            scalar2=None, op0=ALU.mult), reads=[OsbB[e], rsB], writes=[o1B[qt % 2]])
                k.op("dve", lambda: nc.vector.scalar_tensor_tensor(out=osb[e][:, qt, :], in0=Ob[:, 2 + pr, j * 129:j * 129 + 128],
                                                                    scalar=rs[:, 2 + pr, j:j + 1], in1=o1t[qt % 2], op0=ALU.mult, op1=ALU.add),
                     reads=[OsbB[e], rsB, o1B[qt % 2]], writes=[osB[e][qt]])
            st_h = stat_set()
            epi_st[it] = st_h
            for qt in range(4):
                k.op("dve", lambda: nc.vector.scalar_tensor_tensor(out=junkD, in0=osb[e][:, qt, :], scalar=1.0, in1=osb[e][:, qt, :],
                                                                    op0=ALU.mult, op1=ALU.mult, accum_out=st_h[0][:, qt:qt + 1]),
                     reads=[osB[e][qt]], writes=[st_h[4], junkDB])

        def epi2(it):
            h, qc = iters[it]
            e = it % 2
            st_h = epi_st.pop(it)
            Yh = newton(st_h, 4, 1.0 / 128)
            for qt in range(4):
                k.op("dve", lambda: nc.vector.scalar_tensor_tensor(out=a_tm[:, qc * 4 + qt, h * 128:(h + 1) * 128], in0=osb[e][:, qt, :],
                                                                    scalar=Yh[:, qt:qt + 1], in1=gsub, op0=ALU.mult, op1=ALU.mult),
                     reads=[osB[e][qt], st_h[4], gsubB], writes=[aB[qc * 4 + qt][h]])

        qk(0, 0)
        qk(0, 1)
        for it in range(16):
            for kt in range(32):
                softmax_tile(it, kt)
                if it > 0 and kt == 0:
                    epi0(it - 1)
                if it > 0 and kt == 4:
                    epi1(it - 1)
                if it > 0 and kt == 12:
                    epi2(it - 1)
                if kt + 2 < 32:
                    qk(it, kt + 2)
                elif it + 1 < 16:
                    qk(it + 1, kt + 2 - 32)
                av(it, kt)
            if it == 7:
                k.acquire("pool", writes=KTB[0] + KTB[1])
                k.dma("pool", w_out0, w_out0_d.rearrange("(k p) c -> p k c", p=128), reads=[], writes=[w_out0B])
        epi0(15)
        epi1(15)
        epi2(15)
        warm_pe(60)
        k.barrier()
        for qt in range(16):
            bi = next_bank()
            bb = bank_bf(bi)
            k.group("pe", [(lambda hh=hh: nc.tensor.transpose(out=bb[:, hh * 128:(hh + 1) * 128], in_=a_tm[:, qt, hh * 128:(hh + 1) * 128], identity=ident))
                           for hh in range(4)], reads=aB[qt] + [identB], writes=[bankB[bi]])
            evac(aT[:, :, qt * 128:(qt + 1) * 128], bb[:, 0:512].rearrange("p (a b) -> p a b", a=4), [bankB[bi]], [aTB[qt]])
        if debug:
            k.barrier()
            k.dma("sp", dbg["d_a"], a_tm, reads=[KTB[0][0]], sembuf=Buf("dbg"))
            k.barrier()
        if stage <= 3:
            return nc

        k.barrier()
        hres = view(0, [128, 16, 1024], F32)
        hB = [[Buf("h") for _ in range(2)] for _ in range(16)]
        xr = [view(130 + 4 * i, [128, 1024], F32) for i in range(3)]
        xrB = [Buf("xr%d" % i) for i in range(3)]
        for i in range(2):
            k.dma("sp", xr[i], x[i * 128:(i + 1) * 128, :], writes=[xrB[i]])
        stY_box = {}
        for qt in range(16):
            if qt == 9:
                stY_box["ffn0"] = nt_stats(list(range(0, 8)))
            if qt + 2 < 16:
                k.dma("sp", xr[(qt + 2) % 3], x[(qt + 2) * 128:(qt + 3) * 128, :], writes=[xrB[(qt + 2) % 3]])
            for half in range(2):
                bi = next_bank()
                bk = bank(bi)
                hs_ = slice(half * 512, (half + 1) * 512)
                k.group("pe", [(lambda kk=kk: nc.tensor.matmul(bk, lhsT=(fT[:, kk, qt * 128:(qt + 1) * 128] if kk < 4 else aT[:, kk - 4, qt * 128:(qt + 1) * 128]),
                                                               rhs=w_out0[:, kk, hs_], start=(kk == 0), stop=(kk == 7))) for kk in range(8)],
                        reads=[aTB[qt], w_out0B], writes=[bankB[bi]])
                k.op("dve", lambda: nc.vector.tensor_tensor(out=hres[:, qt, hs_], in0=bk, in1=xr[qt % 3][:, hs_], op=ALU.add),
                     reads=[bankB[bi], xrB[qt % 3]], writes=[hB[qt][half]])
        k.barrier()
        def dump_h():
            stB = Buf("st")
            for qt in range(16):
                k.dma("sp", out_d[qt * 128:(qt + 1) * 128, :], hres[:, qt, :], reads=hB[qt], sembuf=stB)
            k.barrier()

        if stage <= 4:
            dump_h()
            return nc

        def next_pair():
            if bank_rr[0] % 2:
                bank_rr[0] += 1
            i = bank_rr[0] % 8
            bank_rr[0] += 2
            return i

        FFB = {}

        def ffn(l, gidx, final=None, pre_down1=None, stY0=None, pre=None):
            hn2 = [view(o_, [128, 1024], BF16) for o_ in (64, 66, 196, 198)]
            gvF = view(68, [128, 1024], F32)
            hnT_ = view(72, [128, 8, 1024], BF16)
            GT = view(88, [128, NJ, 1024], BF16)
            W2 = view(132, [128, NJ, 1024], BF16)
            w13 = [view(176 + 8 * i, [128, 2, 8, 256], BF16) for i in range(2)]
            sil = [view(192 + 2 * i, [128, 512], F32) for i in range(2)]
            hn2B = [Buf("hn2") for _ in range(4)]
            gvFB = FFB.setdefault("gvF", Buf("gvF"))
            hnTB_ = [Buf("hnT") for _ in range(8)]
            if pre is not None:
                hn2B = pre["hn2B"] + hn2B[2:]
                hnTB_ = pre["hnTB_"]
            GTB = [[Buf("GT") for _ in range(2)] for _ in range(NJ)]
            W2B = [FFB.setdefault("W2", Buf("W2"))]
            w13B = FFB.setdefault("w13", [[Buf("w1_%d" % i), Buf("w3_%d" % i)] for i in range(2)])
            silB = [Buf("sil%d" % i) for i in range(2)]
            if pre is None:
                k.dma("sp", gvF, gvecs_d[gidx:gidx + 1, :].partition_broadcast(128), writes=[gvFB])
            npair = NJ // 2
            pair_ctr = [0]
            w2_next = [0]

            def load_pair():
                i = pair_ctr[0]
                pair_ctr[0] += 1
                jp = i % npair
                sl = i % 2
                cs = slice(jp * 256, (jp + 1) * 256)
                if not (i == 0 and pre is not None):
                    k.dma("pool", w13[sl][:, 0, :, :], w1_d[l][:, cs].rearrange("(k p) c -> p k c", p=128), writes=[w13B[sl][0]])
                    k.dma("pool", w13[sl][:, 1, :, :], w3_d[l][:, cs].rearrange("(k p) c -> p k c", p=128), writes=[w13B[sl][1]])
                for _ in range(2):
                    j = w2_next[0]
                    if j < NJ:
                        w2_next[0] += 1
                        k.dma("pool", W2[:, j, :], w2_d[l][j * 128:(j + 1) * 128, :], writes=[W2B[0]], waw=False)

            load_pair()
            load_pair()
            sctr = [0]

            def up(blk):
                for jp in range(npair):
                    sl = (blk * npair + jp) % 2
                    for jj in range(2):
                        j = 2 * jp + jj
                        for tc in range(2):
                            ts = slice(tc * 512, (tc + 1) * 512)
                            b1 = next_bank()
                            b3 = next_bank()
                            k.group("pe", [(lambda kk=kk: nc.tensor.matmul(bank(b1), lhsT=w13[sl][:, 0, kk, jj * 128:(jj + 1) * 128], rhs=hnT_[:, kk, ts], start=(kk == 0), stop=(kk == 7)))
                                           for kk in range(8)], reads=[w13B[sl][0]] + hnTB_[tc * 4:tc * 4 + 4], writes=[bankB[b1]])
                            k.group("pe", [(lambda kk=kk: nc.tensor.matmul(bank(b3), lhsT=w13[sl][:, 1, kk, jj * 128:(jj + 1) * 128], rhs=hnT_[:, kk, ts], start=(kk == 0), stop=(kk == 7)))
                                           for kk in range(8)], reads=[w13B[sl][1]] + hnTB_[tc * 4:tc * 4 + 4], writes=[bankB[b3]])
                            ss_ = sctr[0] % 2
                            sctr[0] += 1
                            k.op("act", lambda: nc.scalar.activation(out=sil[ss_], in_=bank(b1), func=AF.Silu), reads=[bankB[b1]], writes=[silB[ss_]])
                            k.op("dve", lambda: nc.vector.tensor_tensor(out=GT[:, j, ts], in0=bank(b3), in1=sil[ss_], op=ALU.mult),
                                 reads=[bankB[b3], silB[ss_]], writes=[GTB[j][tc]])
                    if pair_ctr[0] < 2 * npair:
                        load_pair()

            def down(blk, hook=None):
                for t in range(8):
                    tile = blk * 8 + t
                    if hook is not None:
                        hook(t)
                    for half in range(2):
                        bi = next_bank()
                        hs_ = slice(half * 512, (half + 1) * 512)
                        k.group("pe", [(lambda j=j: nc.tensor.matmul(bank(bi), lhsT=GT[:, j, t * 128:(t + 1) * 128], rhs=W2[:, j, hs_], start=(j == 0), stop=(j == NJ - 1)))
                                       for j in range(NJ)], reads=[GTB[j][t // 4] for j in range(NJ)] + W2B, writes=[bankB[bi]])
                        k.op("dve", lambda: nc.vector.tensor_tensor(out=hres[:, tile, hs_], in0=bank(bi), in1=hres[:, tile, hs_], op=ALU.add),
                             reads=[bankB[bi]], writes=[hB[tile][half]])

            if pre is None:
                nt_block_now(list(range(0, 8)), gvF, gvFB, hn2, hn2B, hnT_, hnTB_, stY=stY0)
            stY1 = nt_stats(list(range(8, 16)))

            def hook1(t):
                nt_apply(stY1, t, 8 + t, gvF, gvFB, hn2[t % 4], hn2B[t % 4])
                if t >= 1:
                    nt_tr(hn2[(t - 1) % 4], hn2B[(t - 1) % 4], t - 1, hnT_, hnTB_)

            up(0)
            down(0, hook1)
            nt_tr(hn2[3], hn2B[3], 7, hnT_, hnTB_)
            up(1)
            if pre_down1 is not None:
                pre_down1(hn2B + [gvFB] + hnTB_ + [b_ for pr_ in w13B for b_ in pr_] + silB)
            if final is None:
                down(1)
            else:
                fin = final(hn2B + [gvFB] + hnTB_)
                down(1, fin)
                fin(8)
            k.barrier()

        MX = {}

        def carry_of(old_bufs):
            carry = {}
            for ob in old_bufs:
                for ev in ([ob.w] if ob.w is not None else []) + list(ob.r.values()):
                    if ev[2] not in carry or carry[ev[2]][1] < ev[1]:
                        carry[ev[2]] = ev
            return carry

        def mix_pre(dead):
            carry = carry_of(dead)
            MX["w_uv"] = [view(o_, [128, 8, 512], BF16) for o_ in (64, 72, 80, 176)]
            MX["w_o1"] = view(184, [128, 8, 1024], BF16)
            MX["w_uvB"] = [Buf("w_uv%d" % i) for i in range(4)]
            MX["w_o1B"] = Buf("w_o1")
            for nb in MX["w_uvB"] + [MX["w_o1B"]]:
                nb.r = dict(carry)
            for cb in range(4):
                k.dma("pool", MX["w_uv"][cb], w_uv_d[:, cb * 512:(cb + 1) * 512].rearrange("(k p) c -> p k c", p=128), writes=[MX["w_uvB"][cb]])
            k.dma("pool", MX["w_o1"], w_out1_d.rearrange("(k p) c -> p k c", p=128), writes=[MX["w_o1B"]])
            MX["stY0"] = nt_stats(list(range(0, 8)))

        ffn(0, 1, pre_down1=mix_pre, stY0=stY_box.get("ffn0"))
        if stage <= 5:
            dump_h()
            return nc

        w_uv = MX["w_uv"]
        w_o1 = MX["w_o1"]
        hnT1 = view(88, [128, 8, 1024], BF16)
        uT = view(104, [128, 8, 1024], BF16)
        yT = view(120, [128, 8, 1024], BF16)
        vn = [view(136 + 2 * i, [128, 1024], BF16) for i in range(8)]
        wsTf = view(152, [128, 8, 128], F32)
        wsr = [view(156 + 2 * i, [128, 8, 128], BF16) for i in range(2)]
        bsb = view(160, [128, 8, 128], F32)
        gm1 = view(164, [128, 1024], F32)
        hn2m = [view(168 + 2 * i, [128, 1024], BF16) for i in range(2)]
        svt = view(172, [128, 8, 128], F32)
        gvT = small[:, 56:64]
        w_uvB = MX["w_uvB"]
        w_o1B = MX["w_o1B"]
        wsTB, bsbB, gvTB, gm1B = Buf("wsT"), Buf("bsb"), Buf("gvT"), Buf("gm1")
        hnT1B = [Buf("hnT1") for _ in range(8)]
        uTB = [[Buf("uT") for _ in range(2)] for _ in range(8)]
        yTB = [Buf("yT") for _ in range(8)]
        vnB = [[Buf("vn") for _ in range(2)] for _ in range(8)]
        wsrB = [Buf("wsr") for _ in range(2)]
        hn2mB = [Buf("hn2m") for _ in range(2)]
        svtB = Buf("svt")
        k.dma("sp", gm1, gvecs_d[2:3, :].partition_broadcast(128), writes=[gm1B])
        k.dma("sp", bsb.rearrange("p g c -> p (g c)"), bs_d.partition_broadcast(128), writes=[bsbB])
        k.dma("sp", wsTf, wsT_d, writes=[wsTB])
        with nc.allow_non_contiguous_dma(reason="tiny transposed gain vector"):
            k.dma("sp", gvT, gvecs_d[5:6, :].rearrange("o (g c) -> (o c) g", g=8), writes=[gvTB])

        def mix_O(blk, t):
            tile = blk * 8 + t
            for half in range(2):
                bi = next_bank()
                hs_ = slice(half * 512, (half + 1) * 512)
                k.group("pe", [(lambda kk=kk: nc.tensor.matmul(bank(bi), lhsT=yT[:, kk, t * 128:(t + 1) * 128], rhs=w_o1[:, kk, hs_], start=(kk == 0), stop=(kk == 7)))
                               for kk in range(8)], reads=[yTB[t], w_o1B], writes=[bankB[bi]])
                k.op("dve", lambda: nc.vector.tensor_tensor(out=hres[:, tile, hs_], in0=bank(bi), in1=hres[:, tile, hs_], op=ALU.add),
                     reads=[bankB[bi]], writes=[hB[tile][half]])

        nt_block_now(list(range(0, 8)), gm1, gm1B, hn2m, hn2mB, hnT1, hnT1B, stY=MX["stY0"])
        F1 = {}
        for blk in range(2):
            vst = [None, None]
            vY = [None, None]
            stYn = nt_stats(list(range(8, 16))) if blk == 0 else None
            if blk == 1:
                MX["stY_ffn1"] = nt_stats(list(range(0, 8)))
            Spp = {}

            def S_a(t):
                if blk == 0:
                    nt_apply(stYn, t, 8 + t, gm1, gm1B, hn2m[t % 2], hn2mB[t % 2])
                else:
                    nt_apply(MX["stY_ffn1"], t, t, F1["gvF"], FFB["gvF"], F1["hn2"][t % 2], F1["hn2B"][t % 2])
                st_ = vst[t // 4]
                rcol = vY[t // 4][:, t % 4:t % 4 + 1]
                wr = t % 2
                k.op("pool", lambda: nc.gpsimd.tensor_scalar(out=wsr[wr], in0=wsTf, scalar1=rcol, scalar2=1.0, op0=ALU.mult, op1=ALU.mult),
                     reads=[wsTB, st_[4]], writes=[wsrB[wr]])
                pi = next_pair()
                pp = psum[:, pi * 512:(pi + 2) * 512]
                k.group("pe", [(lambda g=g: nc.tensor.matmul(pp[:, g * 128:(g + 1) * 128], lhsT=vn[t][:, g * 128:(g + 1) * 128], rhs=wsr[wr][:, g, :], start=True, stop=True))
                               for g in range(8)], reads=vnB[t] + [wsrB[wr]], writes=[bankB[pi], bankB[pi + 1]])
                k.acquire("act", reads=[bankB[pi], bankB[pi + 1], gvTB], writes=[svtB])
                ev_ = None
                for g in range(8):
                    ev_ = k.op("act", lambda: nc.scalar.activation(out=svt[:, g, :], in_=pp[:, g * 128:(g + 1) * 128], func=AF.Copy, scale=gvT[:, g:g + 1]))
                for b_ in (bankB[pi], bankB[pi + 1]):
                    b_.r[ev_[2]] = ev_
                svtB.w = ev_
                svtB.r = {}
                k.op("dve", lambda: nc.vector.tensor_tensor(out=yT[:, :, t * 128:(t + 1) * 128], in0=svt, in1=bsb, op=ALU.add),
                     reads=[svtB, bsbB], writes=[yTB[t]])

            def S_b(t):
                k.op("dve", lambda: nc.vector.tensor_tensor(out=yT[:, :, t * 128:(t + 1) * 128], in0=yT[:, :, t * 128:(t + 1) * 128],
                                                            in1=uT[:, :, t * 128:(t + 1) * 128], op=ALU.mult),
                     reads=[uTB[c_][t // 4] for c_ in range(8)], writes=[yTB[t]])

            def U_stage(c):
                for tc in range(2):
                    ts = slice(tc * 512, (tc + 1) * 512)
                    bi = next_bank()
                    k.group("pe", [(lambda kk=kk: nc.tensor.matmul(bank(bi), lhsT=w_uv[c // 4][:, kk, (c % 4) * 128:(c % 4 + 1) * 128], rhs=hnT1[:, kk, ts], start=(kk == 0), stop=(kk == 7)))
                                   for kk in range(8)], reads=[w_uvB[c // 4]] + hnT1B[tc * 4:tc * 4 + 4], writes=[bankB[bi]])
                    k.op("act", lambda: nc.scalar.activation(out=uT[:, c, ts], in_=bank(bi), func=AF.Gelu), reads=[bankB[bi]], writes=[uTB[c][tc]])

            def V_stage(t):
                for half in range(2):
                    bi = next_bank()
                    hs_ = slice(half * 512, (half + 1) * 512)
                    k.group("pe", [(lambda kk=kk: nc.tensor.matmul(bank(bi), lhsT=hnT1[:, kk, t * 128:(t + 1) * 128], rhs=w_uv[2 + half][:, kk, :], start=(kk == 0), stop=(kk == 7)))
                                   for kk in range(8)], reads=[w_uvB[2 + half], hnT1B[t]], writes=[bankB[bi]])
                    k.op("act", lambda: nc.scalar.activation(out=vn[t][:, hs_], in_=bank(bi), func=AF.Gelu), reads=[bankB[bi]], writes=[vnB[t][half]])
                if t % 4 == 0:
                    vst[t // 4] = stat_set(dedicated=t // 4)
                sumsq(vst[t // 4], t % 4, vn[t], vnB[t])
                if t % 4 == 3:
                    vY[t // 4] = newton(vst[t // 4], 4, 1.0 / D)

            for t in range(8):
                V_stage(t)
            if blk == 1:
                w13B_ = FFB["w13"]
                k.acquire("pool", writes=[w_uvB[3]])
                f_w13 = view(176, [128, 2, 8, 256], BF16)
                k.dma("pool", f_w13[:, 0, :, :], w1_d[1][:, 0:256].rearrange("(k p) c -> p k c", p=128), writes=[w13B_[0][0]])
                k.dma("pool", f_w13[:, 1, :, :], w3_d[1][:, 0:256].rearrange("(k p) c -> p k c", p=128), writes=[w13B_[0][1]])
            for c in range(8):
                if c == 4 and blk == 1:
                    cr0 = carry_of([w_uvB[0]])
                    F1["gvF"] = view(68, [128, 1024], F32)
                    F1["hn2"] = [view(64, [128, 1024], BF16), view(66, [128, 1024], BF16)]
                    F1["hn2B"] = [Buf("f1hn2a"), Buf("f1hn2b")]
                    for nb in F1["hn2B"]:
                        nb.r = dict(cr0)
                    k.acquire("sp", writes=[w_uvB[0]])
                    k.dma("sp", F1["gvF"], gvecs_d[3:4, :].partition_broadcast(128), writes=[FFB["gvF"]])
                if c == 7:
                    S_a(0)
                U_stage(c)
            S_b(0)
            if blk == 1:
                cr1 = carry_of([w_uvB[1], w_uvB[2]])
                F1["hnT"] = view(72, [128, 8, 1024], BF16)
                F1["hnTB_"] = [Buf("f1hnT") for _ in range(8)]
                for nb in F1["hnTB_"]:
                    nb.r = dict(cr1)
            for t in range(1, 8):
                S_a(t)
                S_b(t)
                if t >= 2:
                    mix_O(blk, t - 2)
                if blk == 0:
                    nt_tr(hn2m[(t - 1) % 2], hn2mB[(t - 1) % 2], t - 1, hnT1, hnT1B)
                else:
                    nt_tr(F1["hn2"][(t - 1) % 2], F1["hn2B"][(t - 1) % 2], t - 1, F1["hnT"], F1["hnTB_"])
            mix_O(blk, 6)
            mix_O(blk, 7)
            if blk == 0:
                nt_tr(hn2m[1], hn2mB[1], 7, hnT1, hnT1B)
            else:
                nt_tr(F1["hn2"][1], F1["hn2B"][1], 7, F1["hnT"], F1["hnTB_"])
        k.barrier()
        if stage <= 6:
            dump_h()
            return nc

        def make_final(old_bufs):
            carry = {}
            for ob in old_bufs:
                for ev in ([ob.w] if ob.w is not None else []) + list(ob.r.values()):
                    if ev[2] not in carry or carry[ev[2]][1] < ev[1]:
                        carry[ev[2]] = ev
            gfin = view(64, [128, 1024], F32)
            ot = [view(68 + 4 * i, [128, 1024], F32) for i in range(4)]
            gfinB = Buf("gfin")
            otB = [Buf("ot%d" % i) for i in range(4)]
            stB = [Buf("st%d" % i) for i in range(4)]
            for nb in [gfinB] + otB:
                nb.r = dict(carry)
            k.dma("sp", gfin, gvecs_d[4:5, :].partition_broadcast(128), writes=[gfinB])
            ctr = [0]

            def hook(t):
                tl = []
                if t < 8:
                    tl.append(t)
                if t >= 1:
                    tl.append(8 + t - 1)
                items = []
                for tile in tl:
                    i = ctr[0] % 4
                    ctr[0] += 1
                    items.append((tile, i))
                rms_batch([(hres[:, tile, :], hB[tile], ot[i], [otB[i]]) for tile, i in items], gfin, gfinB)
                for tile, i in items:
                    k.dma("sp", out_d[tile * 128:(tile + 1) * 128, :], ot[i], reads=[otB[i]], writes=[stB[i]])
            return hook

        ffn(1, 3, final=make_final, stY0=MX.get("stY_ffn1"), pre=dict(hn2B=F1["hn2B"], hnTB_=F1["hnTB_"]))
        k.barrier()
    return nc


def _t5_bucket(rel):
    half, max_exact = 16, 8
    ret = (rel > 0).astype(np.int32) * half
    n = np.abs(rel)
    nf = np.maximum(n, 1).astype(np.float32)
    large = max_exact + (np.log(nf / np.float32(max_exact)) / np.float32(math.log(128 / max_exact))
                         * np.float32(half - max_exact)).astype(np.int32)
    large = np.minimum(large, half - 1)
    return ret + np.where(n < max_exact, n, large)


def _const_tables(hf):
    s_true = np.arange(2048, dtype=np.int64) + hf * 2048
    jj = np.arange(512, dtype=np.int64)
    kk = hf * 2048 + np.stack([4 * jj, 4 * jj + 2, 2 * jj + 1, 1024 + 2 * jj + 1])
    prod = (s_true[None, :, None] * kk[:, None, :]) % S
    ang = prod.astype(np.float64) * (2.0 * np.pi / S)
    dftm = np.stack([np.cos(ang), np.sin(ang)], axis=2) / 64.0
    dftm = np.ascontiguousarray(dftm).astype(ml_dtypes.bfloat16)
    c = np.arange(128, dtype=np.int64)
    a2 = ((c[:, None] * c[None, :]) % 128).astype(np.float64) * (2.0 * np.pi / 128)
    cdft = np.stack([np.cos(a2), -np.sin(a2)], axis=1) / math.sqrt(128.0)
    return dftm, np.ascontiguousarray(cdft).astype(ml_dtypes.bfloat16)


_CONST_CACHE = {}


def _prepare(inputs):
    f32 = np.float32
    g = lambda n: np.asarray(inputs[n], dtype=f32)
    x = g("x")
    rel_table = g("rel_bias_table")
    shared = {
        "w_in": np.ascontiguousarray(g("even_w_in")[0]),
        "w_out0": np.ascontiguousarray(g("even_w_out")[0]),
        "w_uv": np.ascontiguousarray(g("odd_w_uv")[0]),
        "w_out1": np.ascontiguousarray(g("odd_w_out")[0]),
        "gvecs": np.ascontiguousarray(np.stack([g("norm_mix_g")[0], g("norm_ffn_g")[0], g("norm_mix_g")[1],
                                                g("norm_ffn_g")[1], g("final_norm_g"), g("odd_v_norm_g")[0]])),
        "lam": np.ascontiguousarray(g("diff_lambda")[0].reshape(1, 256)),
        "subg": np.ascontiguousarray(g("diff_subln_g")[0].reshape(1, 128)),
        "wsT": np.ascontiguousarray(g("odd_w_s")[0].transpose(2, 0, 1)),
        "bs": np.ascontiguousarray(g("odd_b_s")[0].reshape(1, 1024)),
    }
    for l in range(2):
        shared["w1_%d" % l] = np.ascontiguousarray(g("ffn_w1")[l])
        shared["w3_%d" % l] = np.ascontiguousarray(g("ffn_w3")[l])
        shared["w2_%d" % l] = np.ascontiguousarray(g("ffn_w2")[l])
    p = np.arange(128)[:, None]
    j = np.arange(1152)[None, :]
    qf = np.arange(512)[None, :]
    per_half = []
    for hf in range(2):
        if hf not in _CONST_CACHE:
            _CONST_CACHE[hf] = _const_tables(hf)
        dftm, cdft = _CONST_CACHE[hf]
        wown = rel_table[_t5_bucket(p - j + 512)]
        wown = np.ascontiguousarray(wown.transpose(0, 2, 1))
        sh = 0 if hf == 0 else -S
        wo0 = rel_table[_t5_bucket(512 + p - qf + sh)]
        wo1 = rel_table[_t5_bucket(3968 + p - qf + sh)]
        woth = np.ascontiguousarray(np.stack([wo0, wo1], axis=0).transpose(1, 3, 0, 2))
        bc = np.zeros((128, 12), f32)
        for h in range(4):
            bc[:, 3 * h + 0] = rel_table[15, h]
            bc[:, 3 * h + 1] = rel_table[31, h]
            bc[:, 3 * h + 2] = rel_table[31 if hf == 0 else 15, h]
        per_half.append(dict(dftm=dftm, cdft=cdft, wown=wown, woth=woth, bconst=bc))
    in_maps = []
    for c in range(8):
        b, hf = c // 2, c % 2
        xc = np.concatenate([x[b, hf * 2048:(hf + 1) * 2048], x[b, (1 - hf) * 2048:(2 - hf) * 2048]], axis=0)
        m = dict(shared)
        m.update(per_half[hf])
        m["x"] = np.ascontiguousarray(xc)
        in_maps.append(m)
    return in_maps


_NC_CACHE = {}


def kernel(**inputs):
    in_maps = _prepare(inputs)
    if "nc" not in _NC_CACHE:
        _NC_CACHE["nc"] = build()
    nc = _NC_CACHE["nc"]
    res = run_bass_kernel_spmd(nc, in_maps, core_ids=list(range(8)))
    out = np.empty((NB, S, D), np.float32)
    for c in range(8):
        b, hf = c // 2, c % 2
        out[b, hf * 2048:(hf + 1) * 2048] = np.asarray(res.results[c]["out"], dtype=np.float32)
    return out
```

```python
import math
import numpy as np
import ml_dtypes
from contextlib import ExitStack
import concourse.bass as bass
import concourse.mybir as mybir
from concourse.bass_utils import run_bass_kernel_spmd

F32 = mybir.dt.float32
BF16 = mybir.dt.bfloat16
I32 = mybir.dt.int32
AF = mybir.ActivationFunctionType
ALU = mybir.AluOpType

D = 1024
S = 4096
NB = 4
DFF = 2816
NJ = DFF // 128
EPS = 1e-6
LAM_INIT0 = 0.8 - 0.6 * math.exp(-0.3 * 0)
KIB = 1024


class Buf:
    __slots__ = ("name", "w", "r", "sem", "cnt", "key")

    def __init__(self, name):
        self.name = name
        self.w = None
        self.r = {}
        self.sem = None
        self.cnt = 0
        self.key = None


class KB:
    def __init__(self, nc, es):
        self.nc = nc
        self.es = es
        self.eng = dict(pe=nc.tensor, act=nc.scalar, dve=nc.vector, pool=nc.gpsimd, sp=nc.sync)
        self.esem = {k: es.enter_context(nc.semaphore("e_" + k)) for k in self.eng}
        self.ecnt = {k: 0 for k in self.eng}
        self.seen = {k: {} for k in self.eng}
        self.dbufs = []
        self.load = dict(act=0.0, dve=0.0)

    def _wait(self, e, evs):
        best = {}
        for ev in evs:
            if ev is None:
                continue
            sem, val, key = ev
            if self.seen[e].get(key, 0) >= val:
                continue
            if key not in best or best[key][1] < val:
                best[key] = ev
        for key, (sem, val, _) in best.items():
            self.eng[e].wait_ge(sem, val)
            self.seen[e][key] = val

    @staticmethod
    def _deps(reads, writes):
        evs = []
        for b in reads:
            if b.w is not None:
                evs.append(b.w)
        for b in writes:
            if b.w is not None:
                evs.append(b.w)
            evs.extend(b.r.values())
        return evs

    @staticmethod
    def _commit(ev, reads, writes):
        for b in reads:
            b.r[ev[2]] = ev
        for b in writes:
            b.w = ev
            b.r = {}

    def acquire(self, e, reads=(), writes=()):
        self._wait(e, self._deps(reads, writes))

    def _signal(self, e, inst):
        self.ecnt[e] += 1
        inst.then_inc(self.esem[e], 1)
        return (self.esem[e], self.ecnt[e], e)

    def op(self, e, fn, reads=(), writes=()):
        self._wait(e, self._deps(reads, writes))
        ev = self._signal(e, fn())
        self._commit(ev, reads, writes)
        return ev

    def group(self, e, fns, reads=(), writes=(), acquire=True):
        if acquire:
            self._wait(e, self._deps(reads, writes))
        inst = None
        for fn in fns:
            inst = fn()
        ev = self._signal(e, inst)
        self._commit(ev, reads, writes)
        return ev

    def dma(self, q, out, in_, reads=(), writes=(), sembuf=None, waw=True):
        deps = self._deps(reads, writes)
        if not waw:
            ws = {id(b.w) for b in writes if b.w is not None}
            deps = [e for e in deps if id(e) not in ws]
        self._wait(q, deps)
        sb = sembuf if sembuf is not None else (writes[0] if writes else reads[0])
        if sb.sem is None:
            sb.key = "d%d" % len(self.dbufs)
            sb.sem = self.es.enter_context(self.nc.semaphore("s_" + sb.key))
            self.dbufs.append(sb)
        sb.cnt += 16
        self.eng[q].dma_start(out=out, in_=in_).then_inc(sb.sem, 16)
        ev = (sb.sem, sb.cnt, sb.key)
        self._commit(ev, reads, writes)
        return ev

    def barrier(self):
        evs = [(self.esem[e], self.ecnt[e], e) for e in self.eng if self.ecnt[e] > 0]
        evs += [(b.sem, b.cnt, b.key) for b in self.dbufs if b.cnt > 0]
        for e in self.eng:
            self._wait(e, evs)

    def pick(self, cost):
        e = "act" if self.load["act"] <= self.load["dve"] else "dve"
        self.load[e] += cost
        return e


def build(stage=99, debug=False):
    nc = bass.Bass("TRN2", target_bir_lowering=False)

    def din(name, shape, dt=F32):
        return nc.dram_tensor(name, list(shape), dt, kind="ExternalInput").ap()

    x = din("x", [S, D])
    w_in_d = din("w_in", [D, 2048])
    w_out0_d = din("w_out0", [D, D])
    w_uv_d = din("w_uv", [D, 2048])
    w_out1_d = din("w_out1", [D, D])
    w1_d = [din("w1_%d" % l, [D, DFF]) for l in range(2)]
    w3_d = [din("w3_%d" % l, [D, DFF]) for l in range(2)]
    w2_d = [din("w2_%d" % l, [DFF, D]) for l in range(2)]
    gvecs_d = din("gvecs", [6, D])
    lam_d = din("lam", [1, 256])
    subg_d = din("subg", [1, 128])
    dftm_d = din("dftm", [4, 2048, 2, 512], BF16)
    cdft_d = din("cdft", [128, 2, 128], BF16)
    wown_d = din("wown", [128, 4, 1152])
    woth_d = din("woth", [128, 4, 2, 512])
    bconst_d = din("bconst", [128, 12])
    wsT_d = din("wsT", [128, 8, 128])
    bs_d = din("bs", [1, 1024])
    out_d = nc.dram_tensor("out", [2048, D], F32, kind="ExternalOutput").ap()
    dbg = {}
    if debug:
        for nm, shp, dt in [("d_kt", [128, 4, S], BF16), ("d_qt", [128, 4, 2048], BF16),
                            ("d_v", [128, 32, 4, 132], BF16), ("d_z", [128, 32, 512], BF16),
                            ("d_ft", [128, 4, 2048], BF16), ("d_a", [128, 16, 512], BF16)]:
            dbg[nm] = nc.dram_tensor(nm, shp, dt, kind="ExternalOutput").ap()

    es = ExitStack()
    with es:
        arena = es.enter_context(nc.sbuf_tensor("arena", [128, 207 * 512], BF16))
        psum = es.enter_context(nc.psum_tensor("psum", [128, 4096], F32))
        k = KB(nc, es)

        def view(off_kib, shape, dt):
            esz = 2 if dt == BF16 else 4
            n = 1
            for s_ in shape[1:]:
                n *= s_
            o = int(off_kib * 512)
            a = arena[:, o:o + n * esz // 2]
            if dt == F32:
                a = a.bitcast(F32)
            if len(shape) == 3:
                a = a.rearrange("p (a b) -> p a b", a=shape[1])
            elif len(shape) == 4:
                a = a.rearrange("p (a b c) -> p a b c", a=shape[1], b=shape[2])
            return a

        def bank(i, n=512):
            return psum[:, i * 512:i * 512 + n]

        def bank_bf(i):
            return psum[:, i * 512:(i + 1) * 512].bitcast(BF16)

        bankB = [Buf("bank%d" % i) for i in range(8)]
        bank_rr = [0]

        def next_bank():
            i = bank_rr[0] % 8
            bank_rr[0] += 1
            return i

        def evac(out, in_, reads, writes, scale=None, n=512, eng=None):
            e = eng or k.pick(0.25 + n / 1000.0)
            if e == "act":
                if scale is None:
                    return k.op("act", lambda: nc.scalar.copy(out=out, in_=in_), reads, writes)
                return k.op("act", lambda: nc.scalar.activation(out=out, in_=in_, func=AF.Copy, scale=float(scale)), reads, writes)
            if scale is None:
                return k.op("dve", lambda: nc.vector.tensor_copy(out=out, in_=in_), reads, writes)
            return k.op("dve", lambda: nc.vector.tensor_scalar(out=out, in0=in_, scalar1=float(scale), scalar2=None, op0=ALU.mult), reads, writes)

        ident = view(200, [128, 128], BF16)
        identf = view(200.25, [128, 128], F32)
        junk = view(201, [128, 1024], BF16)
        junkD = view(206, [128, 128], BF16)
        junkAB = Buf("junkA")
        junkDB = Buf("junkD")
        small = view(203, [128, 256], F32)
        gsub = view(204, [128, 128], F32)
        lam_sb = view(204.5, [128, 256], F32)
        bconst = view(205.5, [128, 12], F32)
        identB = Buf("ident")
        smallB = Buf("small")
        gsubB = Buf("gsub")
        lamB = Buf("lam")
        bconstB = Buf("bconst")
        ss_ap = small[:, 0:8]
        rt_ap = small[:, 8:16]
        rstd_ap = small[:, 16:24]
        nlam = small[:, 24:25]
        lsum = small[:, 26:28]
        lexp = small[:, 28:30]
        ltmp = small[:, 30:31]

        k.op("pool", lambda: nc.gpsimd.memset(identf, 0.0), writes=[identB])
        k.op("pool", lambda: nc.gpsimd.affine_select(out=identf, in_=identf, compare_op=ALU.not_equal, fill=1.0,
                                                      base=0, pattern=[[-1, 128]], channel_multiplier=1),
             reads=[identB], writes=[identB])
        k.op("dve", lambda: nc.vector.tensor_copy(out=ident, in_=identf), reads=[identB], writes=[identB])
        k.dma("sp", lam_sb, lam_d.partition_broadcast(128), writes=[lamB])
        k.dma("sp", gsub, subg_d.partition_broadcast(128), writes=[gsubB])
        k.dma("sp", bconst, bconst_d, writes=[bconstB])
        k.op("dve", lambda: nc.vector.scalar_tensor_tensor(out=junkD[:, 0:64], in0=lam_sb[:, 0:64], scalar=1.0, in1=lam_sb[:, 64:128],
                                                            op0=ALU.mult, op1=ALU.mult, accum_out=lsum[:, 0:1]), reads=[lamB], writes=[smallB, junkDB])
        k.op("dve", lambda: nc.vector.scalar_tensor_tensor(out=junkD[:, 64:128], in0=lam_sb[:, 128:192], scalar=1.0, in1=lam_sb[:, 192:256],
                                                            op0=ALU.mult, op1=ALU.mult, accum_out=lsum[:, 1:2]), reads=[lamB, smallB], writes=[smallB, junkDB])
        k.op("act", lambda: nc.scalar.activation(out=lexp, in_=lsum, func=AF.Exp), reads=[smallB], writes=[smallB])
        k.op("dve", lambda: nc.vector.tensor_tensor(out=ltmp, in0=lexp[:, 1:2], in1=lexp[:, 0:1], op=ALU.subtract), reads=[smallB], writes=[smallB])
        k.op("dve", lambda: nc.vector.tensor_scalar(out=nlam, in0=ltmp, scalar1=-LAM_INIT0, scalar2=None, op0=ALU.add), reads=[smallB], writes=[smallB])
        k.op("dve", lambda: nc.vector.tensor_scalar(out=gsub, in0=gsub, scalar1=1.0 - LAM_INIT0, scalar2=None, op0=ALU.mult), reads=[gsubB], writes=[gsubB])

        NSET = 2
        statB = [Buf("stat%d" % i) for i in range(NSET + 4)]
        stat_rr = [0]

        def stat_set(dedicated=None):
            if dedicated is None:
                i = stat_rr[0] % NSET
                stat_rr[0] += 1
            else:
                i = NSET + dedicated
            b0 = 64 + 32 * i
            return (small[:, b0:b0 + 8], small[:, b0 + 8:b0 + 16], small[:, b0 + 16:b0 + 24], small[:, b0 + 24:b0 + 32], statB[i])

        def sumsq(st, col, src, srcB):
            k.op("act", lambda: nc.scalar.activation(out=junk, in_=src, func=AF.Square, accum_out=st[0][:, col:col + 1]), reads=srcB, writes=[st[4], junkAB])

        def newton(st, n, inv_d):
            SS, X, Y, T, sB = st
            SS, X, Y, T = SS[:, 0:n], X[:, 0:n], Y[:, 0:n], T[:, 0:n]
            Xi, Yi = X.bitcast(I32), Y.bitcast(I32)
            d = lambda fn: k.op("dve", fn, reads=[sB], writes=[sB])
            d(lambda: nc.vector.tensor_scalar(out=X, in0=SS, scalar1=float(inv_d), scalar2=EPS, op0=ALU.mult, op1=ALU.add))
            d(lambda: nc.vector.tensor_scalar(out=Yi, in0=Xi, scalar1=1, scalar2=None, op0=ALU.arith_shift_right))
            d(lambda: nc.vector.tensor_scalar(out=Yi, in0=Yi, scalar1=-1.0, scalar2=float(0x5f3759df), op0=ALU.mult, op1=ALU.add))
            for _ in range(3):
                d(lambda: nc.vector.tensor_tensor(out=T, in0=Y, in1=Y, op=ALU.mult))
                d(lambda: nc.vector.tensor_tensor(out=T, in0=T, in1=X, op=ALU.mult))
                d(lambda: nc.vector.tensor_scalar(out=T, in0=T, scalar1=-0.5, scalar2=1.5, op0=ALU.mult, op1=ALU.add))
                d(lambda: nc.vector.tensor_tensor(out=Y, in0=Y, in1=T, op=ALU.mult))
            return Y

        def rms_batch(items, gv, gvB):
            st = stat_set()
            for i, (src_, srcB_, _, _) in enumerate(items):
                sumsq(st, i, src_, srcB_)
            Y = newton(st, len(items), 1.0 / D)
            for i, (src_, srcB_, dst_, dstB_) in enumerate(items):
                k.op("dve", lambda: nc.vector.scalar_tensor_tensor(out=dst_, in0=src_, scalar=Y[:, i:i + 1], in1=gv, op0=ALU.mult, op1=ALU.mult),
                     reads=list(srcB_) + [st[4], gvB], writes=dstB_)

        nt_ctr = [0]

        def nt_stats(tiles):
            st = stat_set(dedicated=2 + nt_ctr[0] % 2)
            nt_ctr[0] += 1
            for i, tile in enumerate(tiles):
                sumsq(st, i, hres[:, tile, :], hB[tile])
            Y = newton(st, len(tiles), 1.0 / D)
            return st, Y

        def nt_apply(stY, i, tile, gv, gvB, slot, slotB):
            st, Y = stY
            k.op("dve", lambda: nc.vector.scalar_tensor_tensor(out=slot, in0=hres[:, tile, :], scalar=Y[:, i:i + 1], in1=gv, op0=ALU.mult, op1=ALU.mult),
                 reads=list(hB[tile]) + [st[4], gvB], writes=[slotB])

        def nt_tr(slot, slotB, t, hnT_, hnTB_):
            bi = next_bank()
            bb = bank_bf(bi)
            k.group("pe", [(lambda kk=kk: nc.tensor.transpose(out=bb[:, kk * 128:(kk + 1) * 128], in_=slot[:, kk * 128:(kk + 1) * 128], identity=ident))
                           for kk in range(8)], reads=[slotB, identB], writes=[bankB[bi]])
            evac(hnT_[:, :, t * 128:(t + 1) * 128], bb.rearrange("p (a b) -> p a b", a=8), [bankB[bi]], [hnTB_[t]], n=1024)

        def nt_block_now(tiles, gv, gvB, slots, slotsB, hnT_, hnTB_, stY=None):
            if stY is None:
                stY = nt_stats(tiles)
            ns = len(slots)
            warm_pe(16)
            for i, tile in enumerate(tiles):
                nt_apply(stY, i, tile, gv, gvB, slots[i % ns], slotsB[i % ns])
                if i >= 1:
                    nt_tr(slots[(i - 1) % ns], slotsB[(i - 1) % ns], i - 1, hnT_, hnTB_)
                    warm_pe(4)
            nt_tr(slots[(len(tiles) - 1) % ns], slotsB[(len(tiles) - 1) % ns], len(tiles) - 1, hnT_, hnTB_)

        eps_t = view(205.75, [128, 2], F32)
        epsB = Buf("eps")
        k.op("dve", lambda: nc.vector.memset(eps_t, EPS), writes=[epsB])
        EPS_AP = eps_t[:, 0:1]

        warm = view(206.25, [128, 256], BF16)
        warmB = Buf("warm")
        k.op("dve", lambda: nc.vector.memset(warm, 0.0), writes=[warmB])

        def warm_pe(n):
            bi = next_bank()
            k.group("pe", [(lambda: nc.tensor.matmul(bank(bi, 256), lhsT=ident, rhs=warm, start=True, stop=True)) for _ in range(n)],
                    reads=[identB, warmB], writes=[bankB[bi]])

        w_in = view(0, [128, 8, 2048], BF16)
        xt = [view(o_, [128, 2, 1024], F32) for o_ in (32, 40, 184)]
        hn = [view(o_, [128, 2, 1024], BF16) for o_ in (48, 52, 192)]
        hnT = [view(56 + 4 * i, [128, 8, 256], BF16) for i in range(2)]
        KT = view(64, [128, 4, S], BF16)
        Vaug = view(96, [128, 32, 4, 132], BF16)
        QT = view(130, [128, 4, 2048], BF16)
        Z = view(146, [128, 32, 512], BF16)
        gvec = view(178, [128, 1024], F32)
        fT = view(184, [128, 4, 2048], BF16)

        w_inB = [Buf("w_in%d" % i) for i in range(4)]
        xtB = [Buf("xt%d" % i) for i in range(3)]
        hnB = [[Buf("hn") for _ in range(2)] for _ in range(3)]
        hnTB = [[Buf("hnT") for _ in range(2)] for _ in range(2)]
        KTB = [[Buf("KT") for _ in range(16)] for _ in range(4)]
        QTB = [[Buf("QT") for _ in range(8)] for _ in range(4)]
        VB = [Buf("V") for _ in range(32)]
        ZB = [Buf("Z") for _ in range(32)]
        gvecB = Buf("gvec")
        onesB = Buf("ones")

        k.dma("sp", gvec, gvecs_d[0:1, :].partition_broadcast(128), writes=[gvecB])
        for cb in (2, 1, 0, 3):
            k.dma("pool", w_in[:, :, cb * 512:(cb + 1) * 512],
                  w_in_d[:, cb * 512:(cb + 1) * 512].rearrange("(k p) c -> p k c", p=128), writes=[w_inB[cb]])
            if cb == 2:
                k.acquire("pool", reads=[w_inB[2]])
        k.op("dve", lambda: nc.vector.memset(Vaug[:, :, :, 128:129], 1.0), writes=[onesB])

        def load_x(g):
            k.dma("sp", xt[g % 3], x[g * 256:(g + 1) * 256, :].rearrange("(t p) d -> p t d", p=128), writes=[xtB[g % 3]])

        def normA(g):
            s3 = g % 3
            rms_batch([(xt[s3][:, t, :], [xtB[s3]], hn[s3][:, t, :], [hnB[s3][t]]) for t in range(2)], gvec, gvecB)

        def transA(g):
            sl = g % 2
            s3 = g % 3
            for t in range(2):
                bi = next_bank()
                bb = bank_bf(bi)
                k.group("pe", [(lambda kk=kk: nc.tensor.transpose(out=bb[:, kk * 128:(kk + 1) * 128], in_=hn[s3][:, t, kk * 128:(kk + 1) * 128], identity=ident))
                               for kk in range(8)], reads=[hnB[s3][t], identB], writes=[bankB[bi]])
                evac(hnT[sl][:, :, t * 128:(t + 1) * 128], bb.rearrange("p (a b) -> p a b", a=8), [bankB[bi]], [hnTB[sl][t]], n=1024)

        load_x(0)
        k.acquire("sp", reads=[xtB[0]])
        load_x(1)
        load_x(2)
        normA(0)
        normA(1)
        transA(0)
        for g in range(16):
            own = g < 8
            sl = g % 2
            if g + 3 < 16:
                load_x(g + 3)
            if g + 2 < 16:
                normA(g + 2)
            hT = hnT[sl]
            hTB = hnTB[sl]
            for which in ("k", "q"):
                if which == "q" and not own:
                    continue
                for h in range(4):
                    c0 = (1024 if which == "k" else 512) + h * 128
                    bi = next_bank()
                    bk = bank(bi, 256)
                    k.group("pe", [(lambda kk=kk: nc.tensor.matmul(bk, lhsT=w_in[:, kk, c0:c0 + 128], rhs=hT[:, kk, :], start=(kk == 0), stop=(kk == 7)))
                                   for kk in range(8)], reads=[w_inB[2 if which == "k" else 1]] + hTB, writes=[bankB[bi]])
                    if which == "k":
                        evac(KT[:, h, g * 256:(g + 1) * 256], bk, [bankB[bi]], [KTB[h][g]], n=256)
                    else:
                        evac(QT[:, h, g * 256:(g + 1) * 256], bk, [bankB[bi]], [QTB[h][g]], scale=0.125, n=256)
            if g + 1 < 16:
                transA(g + 1)
            for t in range(2):
                tile = g * 2 + t
                for which in ("z", "v"):
                    c0 = 0 if which == "z" else 1536
                    bi = next_bank()
                    bk = bank(bi)
                    k.group("pe", [(lambda kk=kk: nc.tensor.matmul(bk, lhsT=hT[:, kk, t * 128:(t + 1) * 128], rhs=w_in[:, kk, c0:c0 + 512], start=(kk == 0), stop=(kk == 7)))
                                   for kk in range(8)], reads=[w_inB[0 if which == "z" else 3], hTB[t]], writes=[bankB[bi]])
                    if which == "z":
                        evac(Z[:, tile, :], bk, [bankB[bi]], [ZB[tile]])
                    else:
                        evac(Vaug[:, tile, :, 0:128], bk.rearrange("p (h e) -> p h e", h=4), [bankB[bi]], [VB[tile]])
        k.barrier()
        if debug:
            k.dma("sp", dbg["d_kt"], KT, reads=[KTB[0][0]], sembuf=Buf("dbg"))
            k.dma("sp", dbg["d_qt"], QT, reads=[KTB[0][0]], sembuf=Buf("dbg"))
            k.dma("sp", dbg["d_v"], Vaug, reads=[KTB[0][0]], sembuf=Buf("dbg"))
            k.dma("sp", dbg["d_z"], Z, reads=[KTB[0][0]], sembuf=Buf("dbg"))
            k.barrier()
        if stage <= 1:
            return nc

        ring = [view(2 * i, [128, 2, 512], BF16) for i in range(16)]
        UV = view(56, [128, 8, 512], BF16)
        cdft = view(182, [128, 2, 128], BF16)
        ringB = [Buf("ring%d" % i) for i in range(16)]
        UVB = [Buf("UV%d" % i) for i in range(8)]
        cdftB = Buf("cdft")
        fTB = [[Buf("fT") for _ in range(4)] for _ in range(4)]
        k.dma("sp", cdft, cdft_d, writes=[cdftB])
        warm_pe(20)

        Zm = view(32, [128, 16, 512], BF16)
        Zpm = view(48, [128, 8, 512], BF16)
        ZmB = [Buf("Zm%d" % i) for i in range(16)]
        ZpmB = [Buf("Zpm%d" % i) for i in range(8)]
        for a in range(16):
            k.op("dve", lambda: nc.vector.tensor_tensor(out=Zm[:, a, :], in0=Z[:, a, :], in1=Z[:, a + 16, :], op=ALU.subtract),
                 reads=[ZB[a], ZB[a + 16]], writes=[ZmB[a]])
            k.op("dve", lambda: nc.vector.tensor_tensor(out=Z[:, a, :], in0=Z[:, a, :], in1=Z[:, a + 16, :], op=ALU.add),
                 reads=[ZB[a + 16]], writes=[ZB[a]])
        for a in range(8):
            k.op("dve", lambda: nc.vector.tensor_tensor(out=Zpm[:, a, :], in0=Z[:, a, :], in1=Z[:, a + 8, :], op=ALU.subtract),
                 reads=[ZB[a], ZB[a + 8]], writes=[ZpmB[a]])
            k.op("dve", lambda: nc.vector.tensor_tensor(out=Z[:, a, :], in0=Z[:, a, :], in1=Z[:, a + 8, :], op=ALU.add),
                 reads=[ZB[a + 8]], writes=[ZB[a]])

        chunk_src = {0: (Z, ZB, 8, slice(0, 2048, 4)), 1: (Zpm, ZpmB, 8, slice(2, 2048, 4)),
                     2: (Zm, ZmB, 16, slice(1, 1024, 2)), 3: (Zm, ZmB, 16, slice(1025, 2048, 2))}
        corder = (2, 3, 0, 1)

        def load_piece(c_, a_):
            k.dma("sp", ring[a_], dftm_d[c_, a_ * 128:(a_ + 1) * 128, :, :], writes=[ringB[a_]])

        for a_ in range(chunk_src[corder[0]][2]):
            load_piece(corder[0], a_)
        hidx = 0
        pending = []
        for ci, c in enumerate(corder):
            zsrc, zsrcB, na, osl = chunk_src[c]
            nxt = corder[ci + 1] if ci + 1 < 4 else None
            for half in range(2):
                b0 = 4 * (hidx % 2)
                hidx += 1
                gs = (2 * half, 2 * half + 1)
                hb = [bankB[b0 + j] for j in range(4)]
                k.acquire("pe", writes=hb)
                ev = None
                for a in range(na):
                    rg = ring[a]
                    fns = []
                    for j, g in enumerate(gs):
                        zl = zsrc[:, a, g * 128:(g + 1) * 128]
                        fns.append(lambda j=j, zl=zl, rg=rg: nc.tensor.matmul(bank(b0 + j), lhsT=zl, rhs=rg[:, 0, :], start=(a == 0), stop=(a == na - 1)))
                        fns.append(lambda j=j, zl=zl, rg=rg: nc.tensor.matmul(bank(b0 + 2 + j), lhsT=zl, rhs=rg[:, 1, :], start=(a == 0), stop=(a == na - 1)))
                    ev = k.group("pe", fns, reads=[ringB[a], zsrcB[a]], writes=[])
                    if half == 1 and nxt is not None and a < chunk_src[nxt][2]:
                        load_piece(nxt, a)
                    if a == 3 and pending:
                        pending.pop(0)()
                for b_ in hb:
                    b_.w = ev
                    b_.r = {}
                for j in range(4):
                    evac(UV[:, b0 + j, :], bank(b0 + j), [bankB[b0 + j]], [UVB[b0 + j]])

                def chan(b0=b0, gs=gs, osl=osl, c=c):
                    for j, g in enumerate(gs):
                        k.group("pe", [lambda j=j: nc.tensor.matmul(bank(b0 + j), lhsT=cdft[:, 0, :], rhs=UV[:, b0 + j, :], start=True, stop=False),
                                       lambda j=j: nc.tensor.matmul(bank(b0 + j), lhsT=cdft[:, 1, :], rhs=UV[:, b0 + 2 + j, :], start=False, stop=True)],
                                reads=[UVB[b0 + j], UVB[b0 + 2 + j], cdftB], writes=[bankB[b0 + j]])
                        evac(fT[:, g, osl], bank(b0 + j), [bankB[b0 + j]], [fTB[g][c]])
                pending.append(chan)
        while pending:
            pending.pop(0)()
        k.barrier()
        if debug:
            k.dma("sp", dbg["d_ft"], fT, reads=[KTB[0][0]], sembuf=Buf("dbg"))
            k.barrier()
        if stage <= 2:
            return nc

        wown = view(0, [128, 4, 1152], F32)
        woth = view(18, [128, 4, 2, 512], F32)
        Pb = [view(34 + 2 * i, [128, 1024], BF16) for i in range(3)]
        Tb = [view(40 + 4 * i, [128, 2, 512], F32) for i in range(2)]
        osb = [view(48 + 2 * i, [128, 4, 128], F32) for i in range(2)]
        o1t = [view(52 + 0.5 * i, [128, 128], F32) for i in range(2)]
        Osb = [view(53 + 4.25 * i, [128, 4, 264], F32) for i in range(2)]
        OsbB = [Buf("Osb%d" % i) for i in range(2)]
        a_tm = view(146, [128, 16, 512], BF16)
        aT = view(162, [128, 4, 2048], BF16)
        w_out0 = view(64, [128, 8, 1024], BF16)
        wownBh = [Buf("wown%d" % i) for i in range(4)]
        wothBh = [Buf("woth%d" % i) for i in range(4)]
        PB = [Buf("P%d" % i) for i in range(3)]
        TB = [[Buf("T%d_%d" % (i, m)) for m in range(2)] for i in range(2)]
        osB = [[Buf("os") for _ in range(4)] for _ in range(2)]
        o1B = [Buf("o1t%d" % i) for i in range(2)]
        aB = [[Buf("a") for _ in range(4)] for _ in range(16)]
        aTB = [Buf("aT%d" % i) for i in range(16)]
        w_out0B = Buf("w_out0")
        LB = [Buf("L0"), Buf("L1")]
        OB = [Buf("O%d" % i) for i in range(4)]
        rsB = Buf("rs")
        hsB = Buf("hs")
        rs = small[:, 32:40].rearrange("p (b j) -> p b j", b=4)
        hss = small[:, 40:44]
        hln = small[:, 44:48]
        hrs = small[:, 48:52]
        for h_ in range(4):
            k.dma("sp", wown[:, h_, :], wown_d[:, h_, :], writes=[wownBh[h_]])
            k.dma("sp", woth[:, h_, :, :], woth_d[:, h_, :, :], writes=[wothBh[h_]])
        Obank = psum[:, 2048:4096].rearrange("p (b c) -> p b c", b=4)
        Lq = [psum[:, 0:1024], psum[:, 1024:2048]]
        iters = [(h_, qc_) for h_ in range(4) for qc_ in range(4)]
        pcount = [0]
        pb_of = {}

        def order_of(qc_):
            s_ = {0: 5, 1: 9, 2: 13, 3: 17}[qc_]
            return [(s_ + p_) % 32 for p_ in range(32)]
        orders = [order_of(qc_) for (_, qc_) in iters]
        Lidx = {}

        def qk(it, pos):
            h, qc = iters[it]
            kt = orders[it][pos]
            i = (it * 32 + pos) % 2
            Lidx[(it, pos)] = i
            ks = slice(kt * 128, (kt + 1) * 128)
            qs = slice(qc * 512, (qc + 1) * 512)
            k.group("pe", [lambda: nc.tensor.matmul(Lq[i][:, 0:512], lhsT=KT[0:64, h, ks], rhs=QT[0:64, h, qs], start=True, stop=True),
                           lambda: nc.tensor.matmul(Lq[i][:, 512:1024], lhsT=KT[64:128, h, ks], rhs=QT[64:128, h, qs], start=True, stop=True)],
                    reads=[KTB[h][kt // 2]] + [QTB[h][2 * qc], QTB[h][2 * qc + 1]], writes=[LB[i]])

        def softmax_tile(it, pos):
            h, qc = iters[it]
            kt = orders[it][pos]
            pb = pcount[0] % 3
            pcount[0] += 1
            pb_of[(it, pos)] = pb
            i = Lidx[(it, pos)]
            argB = None
            if kt < 16:
                dl = kt * 128 - qc * 512
                if -255 < dl < 639:
                    mode, arg, argB = "near", wown[:, h, 512 - dl:1024 - dl], wownBh[h]
                else:
                    col = 3 * h + (1 if dl >= 639 else 0)
                    mode, arg = "far", bconst[:, col:col + 1]
            elif (kt, qc) == (16, 3):
                mode, arg, argB = "near", woth[:, h, 0, :], wothBh[h]
            elif (kt, qc) == (31, 0):
                mode, arg, argB = "near", woth[:, h, 1, :], wothBh[h]
            else:
                mode, arg = "far", bconst[:, 3 * h + 2:3 * h + 3]
            if mode == "far":
                k.op("act", lambda: nc.scalar.activation(out=Pb[pb], in_=Lq[i], func=AF.Exp, bias=arg, scale=1.0),
                     reads=[LB[i], bconstB], writes=[PB[pb]])
            else:
                tb = pos % 2
                for m in range(2):
                    ms = slice(m * 512, (m + 1) * 512)
                    k.op("dve", lambda: nc.vector.tensor_tensor(out=Tb[tb][:, m, :], in0=Lq[i][:, ms], in1=arg, op=ALU.add),
                         reads=[LB[i], argB], writes=[TB[tb][m]])
                k.op("act", lambda: nc.scalar.activation(out=Pb[pb], in_=Tb[tb].rearrange("p m q -> p (m q)"), func=AF.Exp),
                     reads=TB[tb], writes=[PB[pb]])

        def av(it, pos):
            h, qc = iters[it]
            kt = orders[it][pos]
            pb = pb_of[(it, pos)]
            fns = []
            for m in range(2):
                for qt in range(4):
                    bnk = 2 * m + qt // 2
                    j = qt % 2
                    fns.append(lambda m=m, qt=qt, bnk=bnk, j=j: nc.tensor.matmul(
                        Obank[:, bnk, j * 129:(j + 1) * 129], lhsT=Pb[pb][:, m * 512 + qt * 128:m * 512 + (qt + 1) * 128],
                        rhs=Vaug[:, kt, h, 0:129], start=(pos == 0 and j == 0), stop=(pos == 31), skip_group_check=True))
            if pos == 0:
                k.acquire("pe", writes=OB[0:2])
                k.group("pe", fns[0:4], reads=[PB[pb], VB[kt], onesB], writes=[])
                k.acquire("pe", writes=OB[2:4])
                fns = fns[4:]
            ev_ = k.group("pe", fns, reads=[PB[pb], VB[kt], onesB], writes=[])
            if pos == 31:
                for b_ in OB:
                    b_.w = ev_
                    b_.r = {}

        epi_st = {}

        def epi0(it):
            e = it % 2
            for b_ in range(4):
                k.op("dve", lambda: nc.vector.tensor_copy(out=Osb[e][:, b_, 0:258], in_=Obank[:, b_, 0:258]), reads=[OB[b_]], writes=[OsbB[e]])

        def epi1(it):
            e = it % 2
            Ob = Osb[e]
            k.op("dve", lambda: nc.vector.reciprocal(out=rs, in_=Ob[:, :, 128:258:129]), reads=[OsbB[e]], writes=[rsB])
            k.op("dve", lambda: nc.vector.tensor_scalar(out=rs[:, 2:4, :], in0=rs[:, 2:4, :], scalar1=nlam, scalar2=None, op0=ALU.mult),
                 reads=[rsB, smallB], writes=[rsB])
            for qt in range(4):
                pr, j = qt // 2, qt % 2
                k.op("dve", lambda: nc.vector.tensor_scalar(out=o1t[qt % 2], in0=Ob[:, pr, j * 129:j * 129 + 128], scalar1=rs[:, pr, j:j + 1],
                                                            scalar2=None, op0=ALU.mult), reads=[OsbB[e], rsB], writes=[o1B[qt % 2]])
                k.op("dve", lambda: nc.vector.scalar_tensor_tensor(out=osb[e][:, qt, :], in0=Ob[:, 2 + pr, j * 129:j * 129 + 128],
                                                                    scalar=rs[:, 2 + pr, j:j + 1], in1=o1t[qt % 2], op0=ALU.mult, op1=ALU.add),
                     reads=[OsbB[e], rsB, o1B[qt % 2]], writes=[osB[e][qt]])
            st_h = stat_set()
            epi_st[it] = st_h
            for qt in range(4):
                k.op("dve", lambda: nc.vector.scalar_tensor_tensor(out=junkD, in0=osb[e][:, qt, :], scalar=1.0, in1=osb[e][:, qt, :],
                                                                    op0=ALU.mult, op1=ALU.mult, accum_out=st_h[0][:, qt:qt + 1]),
                     reads=[osB[e][qt]], writes=[st_h[4], junkDB])

        def epi2(it):
            h, qc = iters[it]
            e = it % 2
            st_h = epi_st.pop(it)
            Yh = newton(st_h, 4, 1.0 / 128)
            for qt in range(4):
                k.op("dve", lambda: nc.vector.scalar_tensor_tensor(out=a_tm[:, qc * 4 + qt, h * 128:(h + 1) * 128], in0=osb[e][:, qt, :],
                                                                    scalar=Yh[:, qt:qt + 1], in1=gsub, op0=ALU.mult, op1=ALU.mult),
                     reads=[osB[e][qt], st_h[4], gsubB], writes=[aB[qc * 4 + qt][h]])

        qk(0, 0)
        qk(0, 1)
        for it in range(16):
            for kt in range(32):
                softmax_tile(it, kt)
                if it > 0 and kt == 0:
                    epi0(it - 1)
                if it > 0 and kt == 4:
                    epi1(it - 1)
                if it > 0 and kt == 12:
                    epi2(it - 1)
                if kt + 2 < 32:
                    qk(it, kt + 2)
                elif it + 1 < 16:
                    qk(it + 1, kt + 2 - 32)
                av(it, kt)
            if it == 7:
                k.acquire("pool", writes=KTB[0] + KTB[1])
                k.dma("pool", w_out0, w_out0_d.rearrange("(k p) c -> p k c", p=128), reads=[], writes=[w_out0B])
        epi0(15)
        epi1(15)
        epi2(15)
        warm_pe(60)
        k.barrier()
        for qt in range(16):
            bi = next_bank()
            bb = bank_bf(bi)
            k.group("pe", [(lambda hh=hh: nc.tensor.transpose(out=bb[:, hh * 128:(hh + 1) * 128], in_=a_tm[:, qt, hh * 128:(hh + 1) * 128], identity=ident))
                           for hh in range(4)], reads=aB[qt] + [identB], writes=[bankB[bi]])
            evac(aT[:, :, qt * 128:(qt + 1) * 128], bb[:, 0:512].rearrange("p (a b) -> p a b", a=4), [bankB[bi]], [aTB[qt]])
        if debug:
            k.barrier()
            k.dma("sp", dbg["d_a"], a_tm, reads=[KTB[0][0]], sembuf=Buf("dbg"))
            k.barrier()
        if stage <= 3:
            return nc

        k.barrier()
        hres = view(0, [128, 16, 1024], F32)
        hB = [[Buf("h") for _ in range(2)] for _ in range(16)]
        xr = [view(130 + 4 * i, [128, 1024], F32) for i in range(3)]
        xrB = [Buf("xr%d" % i) for i in range(3)]
        for i in range(2):
            k.dma("sp", xr[i], x[i * 128:(i + 1) * 128, :], writes=[xrB[i]])
        stY_box = {}
        for qt in range(16):
            if qt == 9:
                stY_box["ffn0"] = nt_stats(list(range(0, 8)))
            if qt + 2 < 16:
                k.dma("sp", xr[(qt + 2) % 3], x[(qt + 2) * 128:(qt + 3) * 128, :], writes=[xrB[(qt + 2) % 3]])
            for half in range(2):
                bi = next_bank()
                bk = bank(bi)
                hs_ = slice(half * 512, (half + 1) * 512)
                k.group("pe", [(lambda kk=kk: nc.tensor.matmul(bk, lhsT=(fT[:, kk, qt * 128:(qt + 1) * 128] if kk < 4 else aT[:, kk - 4, qt * 128:(qt + 1) * 128]),
                                                               rhs=w_out0[:, kk, hs_], start=(kk == 0), stop=(kk == 7))) for kk in range(8)],
                        reads=[aTB[qt], w_out0B], writes=[bankB[bi]])
                k.op("dve", lambda: nc.vector.tensor_tensor(out=hres[:, qt, hs_], in0=bk, in1=xr[qt % 3][:, hs_], op=ALU.add),
                     reads=[bankB[bi], xrB[qt % 3]], writes=[hB[qt][half]])
        k.barrier()
        def dump_h():
            stB = Buf("st")
            for qt in range(16):
                k.dma("sp", out_d[qt * 128:(qt + 1) * 128, :], hres[:, qt, :], reads=hB[qt], sembuf=stB)
            k.barrier()

        if stage <= 4:
            dump_h()
            return nc

        def next_pair():
            if bank_rr[0] % 2:
                bank_rr[0] += 1
            i = bank_rr[0] % 8
            bank_rr[0] += 2
            return i

        FFB = {}

        def ffn(l, gidx, final=None, pre_down1=None, stY0=None, pre=None):
            hn2 = [view(o_, [128, 1024], BF16) for o_ in (64, 66, 196, 198)]
            gvF = view(68, [128, 1024], F32)
            hnT_ = view(72, [128, 8, 1024], BF16)
            GT = view(88, [128, NJ, 1024], BF16)
            W2 = view(132, [128, NJ, 1024], BF16)
            w13 = [view(176 + 8 * i, [128, 2, 8, 256], BF16) for i in range(2)]
            sil = [view(192 + 2 * i, [128, 512], F32) for i in range(2)]
            hn2B = [Buf("hn2") for _ in range(4)]
            gvFB = FFB.setdefault("gvF", Buf("gvF"))
            hnTB_ = [Buf("hnT") for _ in range(8)]
            if pre is not None:
                hn2B = pre["hn2B"] + hn2B[2:]
                hnTB_ = pre["hnTB_"]
            GTB = [[Buf("GT") for _ in range(2)] for _ in range(NJ)]
            W2B = [FFB.setdefault("W2", Buf("W2"))]
            w13B = FFB.setdefault("w13", [[Buf("w1_%d" % i), Buf("w3_%d" % i)] for i in range(2)])
            silB = [Buf("sil%d" % i) for i in range(2)]
            if pre is None:
                k.dma("sp", gvF, gvecs_d[gidx:gidx + 1, :].partition_broadcast(128), writes=[gvFB])
            npair = NJ // 2
            pair_ctr = [0]
            w2_next = [0]

            def load_pair():
                i = pair_ctr[0]
                pair_ctr[0] += 1
                jp = i % npair
                sl = i % 2
                cs = slice(jp * 256, (jp + 1) * 256)
                if not (i == 0 and pre is not None):
                    k.dma("pool", w13[sl][:, 0, :, :], w1_d[l][:, cs].rearrange("(k p) c -> p k c", p=128), writes=[w13B[sl][0]])
                    k.dma("pool", w13[sl][:, 1, :, :], w3_d[l][:, cs].rearrange("(k p) c -> p k c", p=128), writes=[w13B[sl][1]])
                for _ in range(2):
                    j = w2_next[0]
                    if j < NJ:
                        w2_next[0] += 1
                        k.dma("pool", W2[:, j, :], w2_d[l][j * 128:(j + 1) * 128, :], writes=[W2B[0]], waw=False)

            load_pair()
            load_pair()
            sctr = [0]

            def up(blk):
                for jp in range(npair):
                    sl = (blk * npair + jp) % 2
                    for jj in range(2):
                        j = 2 * jp + jj
                        for tc in range(2):
                            ts = slice(tc * 512, (tc + 1) * 512)
                            b1 = next_bank()
                            b3 = next_bank()
                            k.group("pe", [(lambda kk=kk: nc.tensor.matmul(bank(b1), lhsT=w13[sl][:, 0, kk, jj * 128:(jj + 1) * 128], rhs=hnT_[:, kk, ts], start=(kk == 0), stop=(kk == 7)))
                                           for kk in range(8)], reads=[w13B[sl][0]] + hnTB_[tc * 4:tc * 4 + 4], writes=[bankB[b1]])
                            k.group("pe", [(lambda kk=kk: nc.tensor.matmul(bank(b3), lhsT=w13[sl][:, 1, kk, jj * 128:(jj + 1) * 128], rhs=hnT_[:, kk, ts], start=(kk == 0), stop=(kk == 7)))
                                           for kk in range(8)], reads=[w13B[sl][1]] + hnTB_[tc * 4:tc * 4 + 4], writes=[bankB[b3]])
                            ss_ = sctr[0] % 2
                            sctr[0] += 1
                            k.op("act", lambda: nc.scalar.activation(out=sil[ss_], in_=bank(b1), func=AF.Silu), reads=[bankB[b1]], writes=[silB[ss_]])
                            k.op("dve", lambda: nc.vector.tensor_tensor(out=GT[:, j, ts], in0=bank(b3), in1=sil[ss_], op=ALU.mult),
                                 reads=[bankB[b3], silB[ss_]], writes=[GTB[j][tc]])
                    if pair_ctr[0] < 2 * npair:
                        load_pair()

            def down(blk, hook=None):
                for t in range(8):
                    tile = blk * 8 + t
                    if hook is not None:
                        hook(t)
                    for half in range(2):
                        bi = next_bank()
                        hs_ = slice(half * 512, (half + 1) * 512)
                        k.group("pe", [(lambda j=j: nc.tensor.matmul(bank(bi), lhsT=GT[:, j, t * 128:(t + 1) * 128], rhs=W2[:, j, hs_], start=(j == 0), stop=(j == NJ - 1)))
                                       for j in range(NJ)], reads=[GTB[j][t // 4] for j in range(NJ)] + W2B, writes=[bankB[bi]])
                        k.op("dve", lambda: nc.vector.tensor_tensor(out=hres[:, tile, hs_], in0=bank(bi), in1=hres[:, tile, hs_], op=ALU.add),
                             reads=[bankB[bi]], writes=[hB[tile][half]])

            if pre is None:
                nt_block_now(list(range(0, 8)), gvF, gvFB, hn2, hn2B, hnT_, hnTB_, stY=stY0)
            stY1 = nt_stats(list(range(8, 16)))

            def hook1(t):
                nt_apply(stY1, t, 8 + t, gvF, gvFB, hn2[t % 4], hn2B[t % 4])
                if t >= 1:
                    nt_tr(hn2[(t - 1) % 4], hn2B[(t - 1) % 4], t - 1, hnT_, hnTB_)

            up(0)
            down(0, hook1)
            nt_tr(hn2[3], hn2B[3], 7, hnT_, hnTB_)
            up(1)
            if pre_down1 is not None:
                pre_down1(hn2B + [gvFB] + hnTB_ + [b_ for pr_ in w13B for b_ in pr_] + silB)
            if final is None:
                down(1)
            else:
                fin = final(hn2B + [gvFB] + hnTB_)
                down(1, fin)
                fin(8)
            k.barrier()

        MX = {}

        def carry_of(old_bufs):
            carry = {}
            for ob in old_bufs:
                for ev in ([ob.w] if ob.w is not None else []) + list(ob.r.values()):
                    if ev[2] not in carry or carry[ev[2]][1] < ev[1]:
                        carry[ev[2]] = ev
            return carry

        def mix_pre(dead):
            carry = carry_of(dead)
            MX["w_uv"] = [view(o_, [128, 8, 512], BF16) for o_ in (64, 72, 80, 176)]
            MX["w_o1"] = view(184, [128, 8, 1024], BF16)
            MX["w_uvB"] = [Buf("w_uv%d" % i) for i in range(4)]
            MX["w_o1B"] = Buf("w_o1")
            for nb in MX["w_uvB"] + [MX["w_o1B"]]:
                nb.r = dict(carry)
            for cb in range(4):
                k.dma("pool", MX["w_uv"][cb], w_uv_d[:, cb * 512:(cb + 1) * 512].rearrange("(k p) c -> p k c", p=128), writes=[MX["w_uvB"][cb]])
            k.dma("pool", MX["w_o1"], w_out1_d.rearrange("(k p) c -> p k c", p=128), writes=[MX["w_o1B"]])
            MX["stY0"] = nt_stats(list(range(0, 8)))

        ffn(0, 1, pre_down1=mix_pre, stY0=stY_box.get("ffn0"))
        if stage <= 5:
            dump_h()
            return nc

        w_uv = MX["w_uv"]
        w_o1 = MX["w_o1"]
        hnT1 = view(88, [128, 8, 1024], BF16)
        uT = view(104, [128, 8, 1024], BF16)
        yT = view(120, [128, 8, 1024], BF16)
        vn = [view(136 + 2 * i, [128, 1024], BF16) for i in range(8)]
        wsTf = view(152, [128, 8, 128], F32)
        wsr = [view(156 + 2 * i, [128, 8, 128], BF16) for i in range(2)]
        bsb = view(160, [128, 8, 128], F32)
        gm1 = view(164, [128, 1024], F32)
        hn2m = [view(168 + 2 * i, [128, 1024], BF16) for i in range(2)]
        svt = view(172, [128, 8, 128], F32)
        gvT = small[:, 56:64]
        w_uvB = MX["w_uvB"]
        w_o1B = MX["w_o1B"]
        wsTB, bsbB, gvTB, gm1B = Buf("wsT"), Buf("bsb"), Buf("gvT"), Buf("gm1")
        hnT1B = [Buf("hnT1") for _ in range(8)]
        uTB = [[Buf("uT") for _ in range(2)] for _ in range(8)]
        yTB = [Buf("yT") for _ in range(8)]
        vnB = [[Buf("vn") for _ in range(2)] for _ in range(8)]
        wsrB = [Buf("wsr") for _ in range(2)]
        hn2mB = [Buf("hn2m") for _ in range(2)]
        svtB = Buf("svt")
        k.dma("sp", gm1, gvecs_d[2:3, :].partition_broadcast(128), writes=[gm1B])
        k.dma("sp", bsb.rearrange("p g c -> p (g c)"), bs_d.partition_broadcast(128), writes=[bsbB])
        k.dma("sp", wsTf, wsT_d, writes=[wsTB])
        with nc.allow_non_contiguous_dma(reason="tiny transposed gain vector"):
            k.dma("sp", gvT, gvecs_d[5:6, :].rearrange("o (g c) -> (o c) g", g=8), writes=[gvTB])

        def mix_O(blk, t):
            tile = blk * 8 + t
            for half in range(2):
                bi = next_bank()
                hs_ = slice(half * 512, (half + 1) * 512)
                k.group("pe", [(lambda kk=kk: nc.tensor.matmul(bank(bi), lhsT=yT[:, kk, t * 128:(t + 1) * 128], rhs=w_o1[:, kk, hs_], start=(kk == 0), stop=(kk == 7)))
                               for kk in range(8)], reads=[yTB[t], w_o1B], writes=[bankB[bi]])
                k.op("dve", lambda: nc.vector.tensor_tensor(out=hres[:, tile, hs_], in0=bank(bi), in1=hres[:, tile, hs_], op=ALU.add),
                     reads=[bankB[bi]], writes=[hB[tile][half]])

        nt_block_now(list(range(0, 8)), gm1, gm1B, hn2m, hn2mB, hnT1, hnT1B, stY=MX["stY0"])
        F1 = {}
        for blk in range(2):
            vst = [None, None]
            vY = [None, None]
            stYn = nt_stats(list(range(8, 16))) if blk == 0 else None
            if blk == 1:
                MX["stY_ffn1"] = nt_stats(list(range(0, 8)))
            Spp = {}

            def S_a(t):
                if blk == 0:
                    nt_apply(stYn, t, 8 + t, gm1, gm1B, hn2m[t % 2], hn2mB[t % 2])
                else:
                    nt_apply(MX["stY_ffn1"], t, t, F1["gvF"], FFB["gvF"], F1["hn2"][t % 2], F1["hn2B"][t % 2])
                st_ = vst[t // 4]
                rcol = vY[t // 4][:, t % 4:t % 4 + 1]
                wr = t % 2
                k.op("pool", lambda: nc.gpsimd.tensor_scalar(out=wsr[wr], in0=wsTf, scalar1=rcol, scalar2=1.0, op0=ALU.mult, op1=ALU.mult),
                     reads=[wsTB, st_[4]], writes=[wsrB[wr]])
                pi = next_pair()
                pp = psum[:, pi * 512:(pi + 2) * 512]
                k.group("pe", [(lambda g=g: nc.tensor.matmul(pp[:, g * 128:(g + 1) * 128], lhsT=vn[t][:, g * 128:(g + 1) * 128], rhs=wsr[wr][:, g, :], start=True, stop=True))
                               for g in range(8)], reads=vnB[t] + [wsrB[wr]], writes=[bankB[pi], bankB[pi + 1]])
                k.acquire("act", reads=[bankB[pi], bankB[pi + 1], gvTB], writes=[svtB])
                ev_ = None
                for g in range(8):
                    ev_ = k.op("act", lambda: nc.scalar.activation(out=svt[:, g, :], in_=pp[:, g * 128:(g + 1) * 128], func=AF.Copy, scale=gvT[:, g:g + 1]))
                for b_ in (bankB[pi], bankB[pi + 1]):
                    b_.r[ev_[2]] = ev_
                svtB.w = ev_
                svtB.r = {}
                k.op("dve", lambda: nc.vector.tensor_tensor(out=yT[:, :, t * 128:(t + 1) * 128], in0=svt, in1=bsb, op=ALU.add),
                     reads=[svtB, bsbB], writes=[yTB[t]])

            def S_b(t):
                k.op("dve", lambda: nc.vector.tensor_tensor(out=yT[:, :, t * 128:(t + 1) * 128], in0=yT[:, :, t * 128:(t + 1) * 128],
                                                            in1=uT[:, :, t * 128:(t + 1) * 128], op=ALU.mult),
                     reads=[uTB[c_][t // 4] for c_ in range(8)], writes=[yTB[t]])

            def U_stage(c):
                for tc in range(2):
                    ts = slice(tc * 512, (tc + 1) * 512)
                    bi = next_bank()
                    k.group("pe", [(lambda kk=kk: nc.tensor.matmul(bank(bi), lhsT=w_uv[c // 4][:, kk, (c % 4) * 128:(c % 4 + 1) * 128], rhs=hnT1[:, kk, ts], start=(kk == 0), stop=(kk == 7)))
                                   for kk in range(8)], reads=[w_uvB[c // 4]] + hnT1B[tc * 4:tc * 4 + 4], writes=[bankB[bi]])
                    k.op("act", lambda: nc.scalar.activation(out=uT[:, c, ts], in_=bank(bi), func=AF.Gelu), reads=[bankB[bi]], writes=[uTB[c][tc]])

            def V_stage(t):
                for half in range(2):
                    bi = next_bank()
                    hs_ = slice(half * 512, (half + 1) * 512)
                    k.group("pe", [(lambda kk=kk: nc.tensor.matmul(bank(bi), lhsT=hnT1[:, kk, t * 128:(t + 1) * 128], rhs=w_uv[2 + half][:, kk, :], start=(kk == 0), stop=(kk == 7)))
                                   for kk in range(8)], reads=[w_uvB[2 + half], hnT1B[t]], writes=[bankB[bi]])
                    k.op("act", lambda: nc.scalar.activation(out=vn[t][:, hs_], in_=bank(bi), func=AF.Gelu), reads=[bankB[bi]], writes=[vnB[t][half]])
                if t % 4 == 0:
                    vst[t // 4] = stat_set(dedicated=t // 4)
                sumsq(vst[t // 4], t % 4, vn[t], vnB[t])
                if t % 4 == 3:
                    vY[t // 4] = newton(vst[t // 4], 4, 1.0 / D)

            for t in range(8):
                V_stage(t)
            if blk == 1:
                w13B_ = FFB["w13"]
                k.acquire("pool", writes=[w_uvB[3]])
                f_w13 = view(176, [128, 2, 8, 256], BF16)
                k.dma("pool", f_w13[:, 0, :, :], w1_d[1][:, 0:256].rearrange("(k p) c -> p k c", p=128), writes=[w13B_[0][0]])
                k.dma("pool", f_w13[:, 1, :, :], w3_d[1][:, 0:256].rearrange("(k p) c -> p k c", p=128), writes=[w13B_[0][1]])
            for c in range(8):
                if c == 4 and blk == 1:
                    cr0 = carry_of([w_uvB[0]])
                    F1["gvF"] = view(68, [128, 1024], F32)
                    F1["hn2"] = [view(64, [128, 1024], BF16), view(66, [128, 1024], BF16)]
                    F1["hn2B"] = [Buf("f1hn2a"), Buf("f1hn2b")]
                    for nb in F1["hn2B"]:
                        nb.r = dict(cr0)
                    k.acquire("sp", writes=[w_uvB[0]])
                    k.dma("sp", F1["gvF"], gvecs_d[3:4, :].partition_broadcast(128), writes=[FFB["gvF"]])
                if c == 7:
                    S_a(0)
                U_stage(c)
            S_b(0)
            if blk == 1:
                cr1 = carry_of([w_uvB[1], w_uvB[2]])
                F1["hnT"] = view(72, [128, 8, 1024], BF16)
                F1["hnTB_"] = [Buf("f1hnT") for _ in range(8)]
                for nb in F1["hnTB_"]:
                    nb.r = dict(cr1)
            for t in range(1, 8):
                S_a(t)
                S_b(t)
                if t >= 2:
                    mix_O(blk, t - 2)
                if blk == 0:
                    nt_tr(hn2m[(t - 1) % 2], hn2mB[(t - 1) % 2], t - 1, hnT1, hnT1B)
                else:
                    nt_tr(F1["hn2"][(t - 1) % 2], F1["hn2B"][(t - 1) % 2], t - 1, F1["hnT"], F1["hnTB_"])
            mix_O(blk, 6)
            mix_O(blk, 7)
            if blk == 0:
                nt_tr(hn2m[1], hn2mB[1], 7, hnT1, hnT1B)
            else:
                nt_tr(F1["hn2"][1], F1["hn2B"][1], 7, F1["hnT"], F1["hnTB_"])
        k.barrier()
        if stage <= 6:
            dump_h()
            return nc

        def make_final(old_bufs):
            carry = {}
            for ob in old_bufs:
                for ev in ([ob.w] if ob.w is not None else []) + list(ob.r.values()):
                    if ev[2] not in carry or carry[ev[2]][1] < ev[1]:
                        carry[ev[2]] = ev
            gfin = view(64, [128, 1024], F32)
            ot = [view(68 + 4 * i, [128, 1024], F32) for i in range(4)]
            gfinB = Buf("gfin")
            otB = [Buf("ot%d" % i) for i in range(4)]
            stB = [Buf("st%d" % i) for i in range(4)]
            for nb in [gfinB] + otB:
                nb.r = dict(carry)
            k.dma("sp", gfin, gvecs_d[4:5, :].partition_broadcast(128), writes=[gfinB])
            ctr = [0]

            def hook(t):
                tl = []
                if t < 8:
                    tl.append(t)
                if t >= 1:
                    tl.append(8 + t - 1)
                items = []
                for tile in tl:
                    i = ctr[0] % 4
                    ctr[0] += 1
                    items.append((tile, i))
                rms_batch([(hres[:, tile, :], hB[tile], ot[i], [otB[i]]) for tile, i in items], gfin, gfinB)
                for tile, i in items:
                    k.dma("sp", out_d[tile * 128:(tile + 1) * 128, :], ot[i], reads=[otB[i]], writes=[stB[i]])
            return hook

        ffn(1, 3, final=make_final, stY0=MX.get("stY_ffn1"), pre=dict(hn2B=F1["hn2B"], hnTB_=F1["hnTB_"]))
        k.barrier()
    return nc


def _t5_bucket(rel):
    half, max_exact = 16, 8
    ret = (rel > 0).astype(np.int32) * half
    n = np.abs(rel)
    nf = np.maximum(n, 1).astype(np.float32)
    large = max_exact + (np.log(nf / np.float32(max_exact)) / np.float32(math.log(128 / max_exact))
                         * np.float32(half - max_exact)).astype(np.int32)
    large = np.minimum(large, half - 1)
    return ret + np.where(n < max_exact, n, large)


def _const_tables(hf):
    s_true = np.arange(2048, dtype=np.int64) + hf * 2048
    jj = np.arange(512, dtype=np.int64)
    kk = hf * 2048 + np.stack([4 * jj, 4 * jj + 2, 2 * jj + 1, 1024 + 2 * jj + 1])
    prod = (s_true[None, :, None] * kk[:, None, :]) % S
    ang = prod.astype(np.float64) * (2.0 * np.pi / S)
    dftm = np.stack([np.cos(ang), np.sin(ang)], axis=2) / 64.0
    dftm = np.ascontiguousarray(dftm).astype(ml_dtypes.bfloat16)
    c = np.arange(128, dtype=np.int64)
    a2 = ((c[:, None] * c[None, :]) % 128).astype(np.float64) * (2.0 * np.pi / 128)
    cdft = np.stack([np.cos(a2), -np.sin(a2)], axis=1) / math.sqrt(128.0)
    return dftm, np.ascontiguousarray(cdft).astype(ml_dtypes.bfloat16)


_CONST_CACHE = {}


def _prepare(inputs):
    f32 = np.float32
    g = lambda n: np.asarray(inputs[n], dtype=f32)
    x = g("x")
    rel_table = g("rel_bias_table")
    shared = {
        "w_in": np.ascontiguousarray(g("even_w_in")[0]),
        "w_out0": np.ascontiguousarray(g("even_w_out")[0]),
        "w_uv": np.ascontiguousarray(g("odd_w_uv")[0]),
        "w_out1": np.ascontiguousarray(g("odd_w_out")[0]),
        "gvecs": np.ascontiguousarray(np.stack([g("norm_mix_g")[0], g("norm_ffn_g")[0], g("norm_mix_g")[1],
                                                g("norm_ffn_g")[1], g("final_norm_g"), g("odd_v_norm_g")[0]])),
        "lam": np.ascontiguousarray(g("diff_lambda")[0].reshape(1, 256)),
        "subg": np.ascontiguousarray(g("diff_subln_g")[0].reshape(1, 128)),
        "wsT": np.ascontiguousarray(g("odd_w_s")[0].transpose(2, 0, 1)),
        "bs": np.ascontiguousarray(g("odd_b_s")[0].reshape(1, 1024)),
    }
    for l in range(2):
        shared["w1_%d" % l] = np.ascontiguousarray(g("ffn_w1")[l])
        shared["w3_%d" % l] = np.ascontiguousarray(g("ffn_w3")[l])
        shared["w2_%d" % l] = np.ascontiguousarray(g("ffn_w2")[l])
    p = np.arange(128)[:, None]
    j = np.arange(1152)[None, :]
    qf = np.arange(512)[None, :]
    per_half = []
    for hf in range(2):
        if hf not in _CONST_CACHE:
            _CONST_CACHE[hf] = _const_tables(hf)
        dftm, cdft = _CONST_CACHE[hf]
        wown = rel_table[_t5_bucket(p - j + 512)]
        wown = np.ascontiguousarray(wown.transpose(0, 2, 1))
        sh = 0 if hf == 0 else -S
        wo0 = rel_table[_t5_bucket(512 + p - qf + sh)]
        wo1 = rel_table[_t5_bucket(3968 + p - qf + sh)]
        woth = np.ascontiguousarray(np.stack([wo0, wo1], axis=0).transpose(1, 3, 0, 2))
        bc = np.zeros((128, 12), f32)
        for h in range(4):
            bc[:, 3 * h + 0] = rel_table[15, h]
            bc[:, 3 * h + 1] = rel_table[31, h]
            bc[:, 3 * h + 2] = rel_table[31 if hf == 0 else 15, h]
        per_half.append(dict(dftm=dftm, cdft=cdft, wown=wown, woth=woth, bconst=bc))
    in_maps = []
    for c in range(8):
        b, hf = c // 2, c % 2
        xc = np.concatenate([x[b, hf * 2048:(hf + 1) * 2048], x[b, (1 - hf) * 2048:(2 - hf) * 2048]], axis=0)
        m = dict(shared)
        m.update(per_half[hf])
        m["x"] = np.ascontiguousarray(xc)
        in_maps.append(m)
    return in_maps


_NC_CACHE = {}


def kernel(**inputs):
    in_maps = _prepare(inputs)
    if "nc" not in _NC_CACHE:
        _NC_CACHE["nc"] = build()
    nc = _NC_CACHE["nc"]
    res = run_bass_kernel_spmd(nc, in_maps, core_ids=list(range(8)))
    out = np.empty((NB, S, D), np.float32)
    for c in range(8):
        b, hf = c // 2, c % 2
        out[b, hf * 2048:(hf + 1) * 2048] = np.asarray(res.results[c]["out"], dtype=np.float32)
    return out
```
